# Optimizing a Trainium2 kernel written in Bass

```python
import math
import jax, jax.numpy as jnp
from jax import lax
import numpy as np

D_MODEL = 1024
BATCH = 1
SEQ = 16384
DEPTH = 1
DEC_BATCH = 2
DEC_SEQ = 8192
PAST_LEN = 128

N_GROUPS = 3
WINDOWS = (128, 512, 2048)
DILATIONS = (1, 4, 16)
HEADS_PER_GROUP = 8
HEAD_DIM = 64
ATTN_WIDTH = HEADS_PER_GROUP * HEAD_DIM
QKV_WIDTH = N_GROUPS * ATTN_WIDTH
CONV_WIDTH = 512
CONV_K = 3
N_BRANCH = 2
IN_WIDTH = 3 * QKV_WIDTH + ATTN_WIDTH + 4 * CONV_WIDTH + N_BRANCH * D_MODEL
RMS_EPS = 1e-6
NEG_BIG = -1e30

kernel_name = "hybrid_dilated_attn_shortconv_encoder"


def _alibi_slopes():
    n = N_GROUPS * HEADS_PER_GROUP
    s = 2.0 ** (-8.0 * np.arange(1, n + 1) / n)
    return jnp.asarray(s.astype(np.float32).reshape(N_GROUPS, HEADS_PER_GROUP))


def _split_points():
    sizes = [QKV_WIDTH, QKV_WIDTH, QKV_WIDTH, ATTN_WIDTH,
             CONV_WIDTH, CONV_WIDTH, CONV_WIDTH, CONV_WIDTH, N_BRANCH * D_MODEL]
    return [int(v) for v in np.cumsum(sizes)[:-1]]


def _rmsnorm(x, g):
    xf = x.astype(jnp.float32)
    y = xf * lax.rsqrt(jnp.mean(xf * xf, axis=-1, keepdims=True) + RMS_EPS) * g.astype(jnp.float32)
    return y.astype(x.dtype)


def _dilated_band_attention(q, k, v, dil, radius, slopes):
    B, S, H, Dh = q.shape
    L = S // dil
    blk = radius
    nb = -(-L // blk)
    Lp = nb * blk

    def to_res(t):
        return t.reshape(B, L, dil, H, Dh).transpose(0, 2, 1, 3, 4)

    qr, kr, vr = to_res(q), to_res(k), to_res(v)
    qb = jnp.pad(qr, ((0, 0), (0, 0), (0, Lp - L), (0, 0), (0, 0))).reshape(B, dil, nb, blk, H, Dh)

    def key_windows(t):
        tp = jnp.pad(t, ((0, 0), (0, 0), (blk, blk + Lp - L), (0, 0), (0, 0)))
        tb = tp.reshape(B, dil, nb + 2, blk, H, Dh)
        return jnp.concatenate([tb[:, :, :-2], tb[:, :, 1:-1], tb[:, :, 2:]], axis=3)

    kw, vw = key_windows(kr), key_windows(vr)
    s = jnp.einsum('brnqhd,brnkhd->brhnqk', qb, kw,
                   preferred_element_type=jnp.float32) * (1.0 / math.sqrt(Dh))

    n_idx = jnp.arange(nb)[:, None, None]
    qpos = n_idx * blk + jnp.arange(blk)[None, :, None]
    kpos = (n_idx - 1) * blk + jnp.arange(3 * blk)[None, None, :]
    rel = jnp.abs(kpos - qpos)
    valid = (rel <= radius) & (kpos >= 0) & (kpos < L)
    dist = (rel * dil).astype(jnp.float32)
    bias = -slopes[:, None, None, None] * dist[None]
    s = jnp.where(valid, s + bias, NEG_BIG)

    m = jnp.max(s, axis=-1, keepdims=True)
    p = jnp.exp(s - m)
    den = jnp.sum(p, axis=-1)
    o = jnp.einsum('brhnqk,brnkhd->brnqhd', p, vw.astype(jnp.float32))
    o = o / jnp.transpose(den, (0, 1, 3, 4, 2))[..., None]
    lse = (m[..., 0] + jnp.log(den)).transpose(0, 1, 3, 4, 2)

    o = o.reshape(B, dil, Lp, H, Dh)[:, :, :L].transpose(0, 2, 1, 3, 4).reshape(B, S, H, Dh)
    lse = lse.reshape(B, dil, Lp, H)[:, :, :L].transpose(0, 2, 1, 3).reshape(B, S, H)
    return o, lse


def _layer(x, norm_g, w_in, b_gate, conv_w, w_attn_out, w_conv_out, w_o):
    B, S, _ = x.shape
    hn = _rmsnorm(x, norm_g)
    proj = jnp.einsum('bsd,de->bse', hn, w_in)
    q, k, v, z_a, hc, gb, gc, z_b, g = jnp.split(proj, _split_points(), axis=-1)

    shp = (B, S, N_GROUPS, HEADS_PER_GROUP, HEAD_DIM)
    q, k, v = q.reshape(shp), k.reshape(shp), v.reshape(shp)
    slopes = _alibi_slopes()
    outs, lses = [], []
    for gi in range(N_GROUPS):
        dil = DILATIONS[gi]
        radius = WINDOWS[gi] // (2 * dil)
        o, l = _dilated_band_attention(q[:, :, gi], k[:, :, gi], v[:, :, gi], dil, radius, slopes[gi])
        outs.append(o)
        lses.append(l)
    wts = jax.nn.softmax(jnp.stack(lses), axis=0)
    attn = jnp.einsum('gbsh,gbshd->bshd', wts, jnp.stack(outs))
    attn = attn.reshape(B, S, ATTN_WIDTH).astype(x.dtype)
    a = jnp.einsum('bsc,cd->bsd', jax.nn.silu(z_a) * attn, w_attn_out)

    u = gc * hc
    up = jnp.pad(u, ((0, 0), (1, 1), (0, 0)))
    conv = conv_w[0] * up[:, :-2] + conv_w[1] * up[:, 1:-1] + conv_w[2] * up[:, 2:]
    c = jnp.einsum('bsc,cd->bsd', jax.nn.silu(z_b) * (gb * conv), w_conv_out)

    gates = jax.nn.sigmoid(g + b_gate)
    g_a, g_c = jnp.split(gates, N_BRANCH, axis=-1)
    merged = g_a * a + g_c * c
    return x + jnp.einsum('bsd,de->bse', merged, w_o)


def setup_inputs(seed: int = 0) -> dict:
    key = jax.random.key(seed)
    ks = jax.random.split(key, 10)
    f32 = jnp.float32
    x_prompt = jax.random.normal(ks[0], (BATCH, SEQ, D_MODEL), f32)
    x_sample = jax.random.normal(ks[1], (DEC_BATCH, DEC_SEQ, D_MODEL), f32)
    norm_g = 1.0 + 0.02 * jax.random.normal(ks[2], (DEPTH, D_MODEL), f32)
    w_in = jax.random.normal(ks[3], (DEPTH, D_MODEL, IN_WIDTH), f32) * D_MODEL ** -0.5
    b_gate = 0.01 * jax.random.normal(ks[4], (DEPTH, N_BRANCH * D_MODEL), f32)
    conv_w = jax.random.normal(ks[5], (DEPTH, CONV_K, CONV_WIDTH), f32) * CONV_K ** -0.5
    w_attn_out = jax.random.normal(ks[6], (DEPTH, ATTN_WIDTH, D_MODEL), f32) * ATTN_WIDTH ** -0.5
    w_conv_out = jax.random.normal(ks[7], (DEPTH, CONV_WIDTH, D_MODEL), f32) * CONV_WIDTH ** -0.5
    w_o = jax.random.normal(ks[8], (DEPTH, D_MODEL, D_MODEL), f32) * D_MODEL ** -0.5
    final_g = 1.0 + 0.02 * jax.random.normal(ks[9], (D_MODEL,), f32)
    return {"x_prompt": x_prompt, "x_sample": x_sample, "norm_g": norm_g, "w_in": w_in,
            "b_gate": b_gate, "conv_w": conv_w, "w_attn_out": w_attn_out,
            "w_conv_out": w_conv_out, "w_o": w_o, "final_g": final_g}


def reference(x_prompt, x_sample, norm_g, w_in, b_gate, conv_w, w_attn_out, w_conv_out, w_o, final_g):
    hp, hs = x_prompt, x_sample
    for l in range(DEPTH):
        hp = _layer(hp, norm_g[l], w_in[l], b_gate[l], conv_w[l], w_attn_out[l], w_conv_out[l], w_o[l])
        hs = _layer(hs, norm_g[l], w_in[l], b_gate[l], conv_w[l], w_attn_out[l], w_conv_out[l], w_o[l])
    y_prompt = _rmsnorm(hp, final_g)
    y_sample = _rmsnorm(hs, final_g)
    return (y_prompt, y_sample)
```

```python
import numpy as np
import ml_dtypes
import concourse.bass as bass
import concourse.mybir as mybir
from concourse.bass_utils import run_bass_kernel_spmd

F32 = mybir.dt.float32
BF16 = mybir.dt.bfloat16
AF = mybir.ActivationFunctionType
ALU = mybir.AluOpType

NCORES = 8
D = 1024
INW = 9216
TOK_CORE = 4096
NHALF = 2
NOWN = 2048
WIN = 4096
LW = 256
XROWS = 6144
EPS = 1e-6
DIL = (1, 4, 16)

C_Q, C_K, C_V, C_ZA, C_HC, C_GB, C_GC, C_ZB, C_G = 0, 1536, 3072, 4608, 5120, 5632, 6144, 6656, 7168

X_BYTES = 84224
NWS = 8
EMBED_WAIT = ("pe", "act", "dve", "pool")
VB_T = 192


def _slopes():
    n = 24
    s = 2.0 ** (-8.0 * np.arange(1, n + 1) / n)
    return s.astype(np.float32).reshape(3, 8)


def _tile_off(g):
    idx = np.arange(128)
    if g == 0:
        return 16 * (idx % 8) + idx // 8
    if g == 1:
        return 4 * (idx % 32) + idx // 32
    return idx


def _etab():
    sl = _slopes()
    out = np.zeros((24, 128, 256), np.float32)
    for g in range(3):
        off = _tile_off(g).astype(np.int64)
        for t, base in enumerate((-64, 64)):
            delta = (off[:, None] + base) - off[None, :]
            ad = np.abs(delta)
            valid = ad <= 64
            for h in range(8):
                e = np.exp((-sl[g, h] * DIL[g]) * ad.astype(np.float32)).astype(np.float32)
                out[g * 8 + h, :, t * 128:(t + 1) * 128] = np.where(valid, e, 0.0)
    return out.astype(ml_dtypes.bfloat16)


def _patterns():
    idx = np.arange(128)
    p = np.zeros((128, 6, 64), np.float32)
    p[:, 0, :] = ((idx % 8) < 4)[:, None]
    p[:, 1, :] = ((idx % 32) < 16)[:, None]
    p[:, 2, :] = (idx < 64)[:, None]
    p[:, 3, :] = ((idx % 8) >= 4)[:, None]
    p[:, 4, :] = ((idx % 32) >= 16)[:, None]
    p[:, 5, :] = (idx >= 64)[:, None]
    return p.astype(ml_dtypes.bfloat16).reshape(128, 6 * 64)


class Res:
    __slots__ = ("ready", "free", "name")

    def __init__(self, name=""):
        self.ready = {}
        self.free = {}
        self.name = name


def _merge(dst, src):
    for k, v in src.items():
        if dst.get(k, 0) < v:
            dst[k] = v


class Eng:
    def __init__(self, name, semname, is_pe=False):
        self.name = name
        self.semname = semname
        self.count = 0
        self.waited = {}
        self.prog = []
        self.is_pe = is_pe


class View:
    def __init__(self, ap2d):
        self.t = ap2d.tensor
        self.off = ap2d.offset
        self.ps = ap2d.ap[0][0]
        self.w = ap2d.ap[1][1]
        self.base = ap2d

    def ap(self, col, dims, p0=0, npart=128):
        return bass.AP(self.t, self.off + p0 * self.ps + col, [[self.ps, npart]] + [list(d) for d in dims])

    def cols(self, c0, n, p0=0, npart=128):
        return self.ap(c0, [[1, n]], p0, npart)


class Builder:
    def __init__(self):
        self.nc = bass.Bass("TRN2", target_bir_lowering=False)
        self.engs = {
            "pe": Eng("pe", "s_pe", True),
            "act": Eng("act", "s_act"),
            "dve": Eng("dve", "s_dve"),
            "pool": Eng("pool", "s_pool"),
            "sp": Eng("sp", "s_sp"),
        }
        self.dma_counts = {}
        self.epoch = {}
        self.semnames = ["s_pe", "s_act", "s_dve", "s_pool"]

    def _waits(self, E, reads, writes, noepoch):
        waits = {} if (noepoch or E.is_pe) else dict(self.epoch)
        for r in reads:
            _merge(waits, r.ready)
        for w in writes:
            _merge(waits, w.ready)
            _merge(waits, w.free)
        wl = []
        for s, v in waits.items():
            if E.is_pe and s == E.semname:
                continue
            if E.waited.get(s, 0) < v:
                wl.append((s, v))
                E.waited[s] = v
        return wl

    def op(self, eng, fn, reads=(), writes=(), signal=True, noepoch=False):
        E = self.engs[eng]
        wl = self._waits(E, reads, writes, noepoch)
        tok = None
        if signal:
            E.count += 1
            tok = (E.semname, E.count)
        E.prog.append((wl, fn, tok, 1))
        if tok:
            for r in reads:
                if r.free.get(tok[0], 0) < tok[1]:
                    r.free[tok[0]] = tok[1]
            for w in writes:
                if w.ready.get(tok[0], 0) < tok[1]:
                    w.ready[tok[0]] = tok[1]
        return tok

    def pe_group(self, fns, reads=(), writes=()):
        E = self.engs["pe"]
        wl = self._waits(E, reads, writes, False)
        E.count += 1
        tok = (E.semname, E.count)
        n = len(fns)
        for i, fn in enumerate(fns):
            E.prog.append((wl if i == 0 else [], fn, tok if i == n - 1 else None, 1))
        for r in reads:
            if r.free.get(tok[0], 0) < tok[1]:
                r.free[tok[0]] = tok[1]
        for w in writes:
            if w.ready.get(tok[0], 0) < tok[1]:
                w.ready[tok[0]] = tok[1]
        return tok

    def dma(self, queue, semname, out, in_, reads=(), writes=(), noepoch=False):
        E = self.engs[queue]
        wl = self._waits(E, reads, writes, noepoch)
        if semname not in self.dma_counts:
            self.dma_counts[semname] = 0
            self.semnames.append(semname)
        self.dma_counts[semname] += 16
        tok = (semname, self.dma_counts[semname])
        E.prog.append((wl, lambda e: e.dma_start(out=out, in_=in_), tok, 16))
        for r in reads:
            if r.free.get(tok[0], 0) < tok[1]:
                r.free[tok[0]] = tok[1]
        for w in writes:
            if w.ready.get(tok[0], 0) < tok[1]:
                w.ready[tok[0]] = tok[1]
        return tok

    def barrier(self):
        ep = {}
        for k in ("pe", "act", "dve", "pool"):
            E = self.engs[k]
            if E.count:
                ep[E.semname] = E.count
        for s, v in self.dma_counts.items():
            ep[s] = v
        self.epoch = ep


def build_program():
    B = Builder()
    nc = B.nc
    op, pe_group, dma = B.op, B.pe_group, B.dma

    xw = nc.dram_tensor("xw", [XROWS, D], F32, kind="ExternalInput")
    w_in = nc.dram_tensor("w_in", [D, INW], F32, kind="ExternalInput")
    w_ao = nc.dram_tensor("w_ao", [512, D], F32, kind="ExternalInput")
    w_co = nc.dram_tensor("w_co", [512, D], F32, kind="ExternalInput")
    w_o = nc.dram_tensor("w_o", [D, D], F32, kind="ExternalInput")
    gvec = nc.dram_tensor("gvec", [D], F32, kind="ExternalInput")
    fgvec = nc.dram_tensor("fgvec", [D], F32, kind="ExternalInput")
    cfp_d = nc.dram_tensor("cfp", [128, 32], F32, kind="ExternalInput")
    cbf_d = nc.dram_tensor("cbf", [128, 512], BF16, kind="ExternalInput")
    et_d = nc.dram_tensor("etab", [24 * 128, 256], BF16, kind="ExternalInput")
    y_d = nc.dram_tensor("y", [TOK_CORE, D], F32, kind="ExternalOutput")

    import contextlib
    es = contextlib.ExitStack()
    with es:
        def sb(name, shape, dt):
            return es.enter_context(nc.sbuf_tensor(name, shape, dt))

        hnT_t = sb("hnT", [128, 8 * WIN], BF16)
        wbf_t = sb("wbf", [128, NWS * 1024], BF16)
        et_t = sb("et", [128, 2 * 1536], BF16)
        gattn_t = sb("gattn", [128, 4 * NOWN], BF16)
        gconv_t = sb("gconv", [128, 4 * NOWN], BF16)
        X_t = sb("X", [128, X_BYTES // 2], BF16)
        cf_t = sb("cf", [128, 64], F32)
        stat_t = sb("stat", [128, 32], F32)
        cbf_t = sb("cbfs", [128, 512], BF16)
        patt_t = sb("patt", [128, 384], BF16)
        ps_t = es.enter_context(nc.psum_tensor("ps", [128, 4096], F32))
        sem = {}
        for sname in ["s_pe", "s_act", "s_dve", "s_pool"]:
            sem[sname] = es.enter_context(nc.semaphore(sname))
        dma_sem_names = (["d_w%d" % i for i in range(NWS)] + ["d_e0", "d_e1"] + ["d_x1%d" % i for i in range(4)] + ["d_x3%d" % i for i in range(5)]
                         + ["d_y%d" % i for i in range(5)] + ["d_wao", "d_wco", "d_c", "d_gb", "d_fg"]
                         + ["d_xb%d" % i for i in range(8)])
        for sname in dma_sem_names:
            sem[sname] = es.enter_context(nc.semaphore(sname))

        hnT = View(hnT_t[:])
        wbf = View(wbf_t[:])
        etv = View(et_t[:])
        gattn = View(gattn_t[:])
        gconv = View(gconv_t[:])
        cf = View(cf_t[:])
        stat = View(stat_t[:])
        patc = View(cbf_t[:, 0:384])
        patt = View(patt_t[:])
        ident = View(cbf_t[:, 384:512])
        PS = View(ps_t[:])

        def xr(boff, nbytes, dt):
            a = X_t[:, boff // 2:(boff + nbytes) // 2]
            if dt == F32:
                a = a.bitcast(F32)
            return View(a)

        def bank(b, nb=1):
            return View(ps_t[:, 512 * b:512 * (b + nb)])

        def bank_bf(b):
            return View(ps_t[:, 512 * b:512 * (b + 1)].bitcast(BF16))

        R_bank = [Res("bank%d" % i) for i in range(8)]
        R_hnT = Res("hnT")
        R_wbf = [Res() for _ in range(NWS)]
        R_et = [Res(), Res()]
        R_gattn = [Res() for _ in range(4)]
        R_gconv = [Res() for _ in range(4)]
        R_const = Res("const")
        R_stat = [Res() for _ in range(8)]
        R_patt = Res()

        mergedV = xr(0, 32768, BF16)
        fgV = xr(32768, 4096, F32)
        gbV = xr(36864, 4096, F32)
        xt3V = [xr(40960 + 4096 * i, 4096, F32) for i in range(5)]
        xt1V = [xr(61440 + 4096 * i, 4096, F32) for i in range(4)]
        xnV = [xr(77824 + 2048 * i, 2048, BF16) for i in range(2)]
        xt1bV = [xr(4096 * i, 4096, F32) for i in range(8)]
        junkV = xr(81920, 2048, BF16)
        qtV = [xr(0 + 4096 * i, 4096, BF16) for i in range(2)]
        ktV = [xr(8192 + 8192 * i, 8192, BF16) for i in range(2)]
        vbV = [xr(24576 + 12288 * i, 12288, BF16) for i in range(2)]
        vbV32 = [xr(24576 + 12288 * i, 12288, F32) for i in range(2)]
        accV = xr(49152, 16384, F32)
        prV = [xr(65536 + 2048 * i, 2048, BF16) for i in range(3)]
        rscV = [xr(71680 + 2048 * i, 2048, F32) for i in range(2)]
        thV = [xr(75776 + 2048 * i, 2048, F32) for i in range(2)]
        vtsV = [xr(79872 + 1024 * i, 1024, BF16) for i in range(2)]
        vt0V = xr(79872, 4352, BF16)
        uV = xr(0, 8320, F32)
        caV = xr(8320, 8192, F32)
        gzV = xr(16512, 8192, F32)
        hcsV = [xr(24704 + 2048 * i, 2048, F32) for i in range(2)]
        thbV = [xr(28800 + 2048 * i, 2048, F32) for i in range(2)]
        waoV = xr(36864, 8192, BF16)
        wcoV = xr(45056, 8192, BF16)
        tgaV = [xr(53248 + 2048 * i, 2048, F32) for i in range(2)]
        tgcV = [xr(57344 + 2048 * i, 2048, F32) for i in range(2)]

        CF_HB, CF_CW, CF_FL, CF_BG, CF_MH = 0, 16, 28, 32, 48

        def load_consts():
            dma("sp", "d_c", View(cbf_t[:]).base, cbf_d.ap(), writes=[R_const])
            dma("sp", "d_c", cf.cols(CF_CW, 32), cfp_d.ap(), writes=[R_const])
            op("dve", lambda e: e.memset(cf.cols(CF_MH, 1), -0.5), writes=[R_const])
            op("dve", lambda e: e.tensor_scalar(out=cf.cols(CF_HB, 16), in0=cf.cols(CF_BG, 16), scalar1=0.5,
                                                scalar2=None, op0=ALU.mult), reads=[R_const], writes=[R_const])

        wstate = {"issued": 0, "blocks": []}

        def wblocks_for_half():
            bl = []
            for hp in range(4):
                for g in range(3):
                    for c in (C_Q, C_K, C_V):
                        bl.append(c + g * 512 + hp * 128)
                bl.append(C_ZA + hp * 128)
            for ct in range(4):
                for c in (C_HC, C_GC, C_GB, C_ZB):
                    bl.append(c + ct * 128)
            for dt_ in range(8):
                bl.append(C_G + dt_ * 128)
                bl.append(C_G + 1024 + dt_ * 128)
            return bl

        wstate["blocks"] = wblocks_for_half() + wblocks_for_half()

        def wprefetch(upto):
            upto = min(upto, len(wstate["blocks"]) - 1)
            while wstate["issued"] <= upto:
                i = wstate["issued"]
                col = wstate["blocks"][i]
                s = i % NWS
                src = bass.AP(w_in, col, [[INW, 128], [128 * INW, 8], [1, 128]])
                dst = wbf.ap(s * 1024, [[128, 8], [1, 128]])
                dma("pool", "d_w%d" % s, dst, src, writes=[R_wbf[s]], noepoch=True)
                wstate["issued"] += 1

        wctr = {"i": 0}

        def wnext(expect_col):
            i = wctr["i"]
            assert wstate["blocks"][i] == expect_col, (i, wstate["blocks"][i], expect_col)
            assert wstate["issued"] > i, "weight block not prefetched"
            assert wstate["issued"] - i <= NWS
            wctr["i"] += 1
            return i % NWS

        wlimit = {"i": 10 ** 9}

        def wahead(n):
            wprefetch(min(wctr["i"] + n - 1, wlimit["i"]))
            assert wstate["issued"] - wctr["i"] <= NWS

        def w_lhsT(s, kt):
            return wbf.cols(s * 1024 + kt * 128, 128)

        evac_flip = {"i": 0}

        def evac_engine():
            evac_flip["i"] += 1
            return "act" if evac_flip["i"] % 2 else "dve"

        def copy_op(eng, out, in_, reads, writes):
            if eng == "act":
                return op("act", lambda e: e.activation(out=out, in_=in_, func=AF.Copy), reads=reads, writes=writes)
            return op("dve", lambda e: e.tensor_copy(out=out, in_=in_), reads=reads, writes=writes)

        def proj_fm(bk, s, rhs_start, rhs_dims, n, extra_reads=()):
            bv = bank(bk)
            fns = []
            for kt in range(8):
                fns.append(lambda e, kt=kt: e.matmul(bv.cols(0, n), lhsT=w_lhsT(s, kt),
                                                     rhs=hnT.ap(kt * WIN + rhs_start, rhs_dims),
                                                     start=(kt == 0), stop=(kt == 7)))
            return pe_group(fns, reads=[R_wbf[s], R_hnT] + list(extra_reads), writes=[R_bank[bk]])

        stat_ctr = {"i": 0}

        def rms_stat(src_ap, src_res, junk_ap, junk_res):
            k = stat_ctr["i"] % 8
            stat_ctr["i"] += 1
            r = R_stat[k]
            c0 = 3 * k
            op("act", lambda e: e.activation(out=junk_ap, in_=src_ap, func=AF.Square,
                                             accum_out=stat.cols(c0, 1)),
               reads=[src_res], writes=[junk_res, r])
            op("pool", lambda e: e.tensor_scalar(out=stat.cols(c0 + 1, 1), in0=stat.cols(c0, 1), scalar1=1.0 / D,
                                                 scalar2=EPS, op0=ALU.mult, op1=ALU.add), reads=[r], writes=[r])
            op("pool", lambda e: e.tensor_tensor(out=stat.cols(c0 + 2, 1), in0=stat.cols(c0 + 1, 1),
                                                 in1=cf.cols(CF_MH, 1), op=ALU.pow), reads=[r, R_const], writes=[r])
            return stat.cols(c0 + 2, 1), r

        R_junk = Res("junk")
        R_xt1 = [Res() for _ in range(4)]
        R_xt1b = [Res() for _ in range(8)]
        R_xn = [Res(), Res()]
        R_xt3 = [Res() for _ in range(5)]
        R_gb = Res()
        R_fg = Res()
        R_merged = [Res() for _ in range(8)]
        R_wo = Res()
        R_wao = Res()
        R_wco = Res()

        p1ctr = {"i": 0}

        def p1_stages(h, ti):
            r16, lb = ti // 2, ti % 2
            i = p1ctr["i"]
            p1ctr["i"] += 1
            if h == 0:
                s = i % 8
                xv, xres, xsem = xt1bV[s], R_xt1b[s], "d_xb%d" % s
            else:
                s = i % 4
                xv, xres, xsem = xt1V[s], R_xt1[s], "d_x1%d" % s
            sn = i % 2
            bk = (i % 4) if h == 0 else 4 + (i % 2)
            st = {}

            def L():
                row0 = 2048 * h + 2048 * lb + r16
                src = bass.AP(xw, row0 * D, [[16 * D, 128], [1, D]])
                dma("sp", xsem, xv.base, src, writes=[xres])

            def S():
                st["rstd"], st["rs"] = rms_stat(xv.base, xres, junkV.base, R_junk)

            def N():
                rstd, rs = st["rstd"], st["rs"]
                op("dve", lambda e: e.scalar_tensor_tensor(out=xnV[sn].base, in0=xv.base, scalar=rstd,
                                                           in1=gbV.base, op0=ALU.mult, op1=ALU.mult),
                   reads=[xres, rs, R_gb], writes=[R_xn[sn]])
                pb = bank_bf(bk)
                fns = []
                for kt in range(8):
                    fns.append(lambda e, kt=kt: e.transpose(out=pb.cols(kt * 128, 128),
                                                            in_=xnV[sn].cols(kt * 128, 128), identity=ident.base))
                pe_group(fns, reads=[R_xn[sn], R_const], writes=[R_bank[bk]])

            def E():
                pb = bank_bf(bk)
                pos0 = r16 * LW + 128 * lb
                copy_op("dve" if (h == 0 and i % 2) else "act", hnT.ap(pos0, [[WIN, 8], [1, 128]]),
                        pb.ap(0, [[128, 8], [1, 128]]), reads=[R_bank[bk]], writes=[R_hnT])
            return [(0, L), (1, S), (3, N), (6, E)] if h == 0 else [(0, L), (1, S), (3, N), (4.5, E)]

        def run_pipes(pipes):
            ev = []
            for pi_, (tiles, rate) in enumerate(pipes):
                for ti, stages in enumerate(tiles):
                    for (skew, fn) in stages:
                        ev.append((ti / rate + skew, pi_, ti, skew, fn))
            ev.sort(key=lambda x: x[:4])
            for e_ in ev:
                e_[4]()

        def load_gb():
            dma("sp", "d_gb", gbV.base, bass.AP(gvec, 0, [[0, 128], [1, D]]), writes=[R_gb])

        def load_fg():
            dma("sp", "d_fg", fgV.base, bass.AP(fgvec, 0, [[0, 128], [1, D]]), writes=[R_fg])

        p3ctr = {"i": 0}

        def load_wo(rng):
            for dt_ in rng:
                src = bass.AP(w_o, dt_ * 128 * D, [[D, 128], [1, D]])
                dma("pool", "d_w%d" % dt_, wbf.cols(dt_ * 1024, 1024), src, writes=[R_wbf[dt_]])

        def p3_stages(h, r16):
            i = p3ctr["i"]
            p3ctr["i"] += 1
            s = i % 5
            b0 = 2 * (i % 4) if h == NHALF - 1 else 2 * (i % 2)
            st = {}

            def LM():
                row0 = 2048 * h + 1024 + r16
                src = bass.AP(xw, row0 * D, [[16 * D, 128], [1, D]])
                dma("sp", "d_x3%d" % s, xt3V[s].base, src, writes=[R_xt3[s]])
                for nh in range(2):
                    bv = bank(b0 + nh)
                    fns = []
                    for dt_ in range(8):
                        fns.append(lambda e, dt_=dt_, nh=nh, bv=bv: e.matmul(
                            bv.cols(0, 512), lhsT=mergedV.cols(dt_ * NOWN + r16 * 128, 128),
                            rhs=wbf.cols(dt_ * 1024 + nh * 512, 512), start=(dt_ == 0), stop=(dt_ == 7)))
                    pe_group(fns, reads=R_merged + R_wbf, writes=[R_bank[b0 + nh]])

            def A():
                b2 = bank(b0, 2)
                op("dve", lambda e: e.scalar_tensor_tensor(out=xt3V[s].base, in0=b2.base, scalar=0.5,
                                                           in1=xt3V[s].base, op0=ALU.mult, op1=ALU.add),
                   reads=[R_bank[b0], R_bank[b0 + 1], R_xt3[s]], writes=[R_xt3[s]])

            def S3():
                st["rstd"], st["rs"] = rms_stat(xt3V[s].base, R_xt3[s], junkV.base, R_junk)

            def F():
                rstd, rs = st["rstd"], st["rs"]
                op("dve", lambda e: e.scalar_tensor_tensor(out=xt3V[s].base, in0=xt3V[s].base, scalar=rstd,
                                                           in1=fgV.base, op0=ALU.mult, op1=ALU.mult),
                   reads=[R_xt3[s], rs, R_fg], writes=[R_xt3[s]])

            def ST():
                orow0 = 2048 * h + r16
                dst = bass.AP(y_d, orow0 * D, [[16 * D, 128], [1, D]])
                dma("sp", "d_y%d" % s, dst, xt3V[s].base, reads=[R_xt3[s]])
            return [(0, LM), (1, A), (2, S3), (3, F), (4, ST)]

        R_qt = [Res(), Res()]
        R_kt = [Res(), Res()]
        R_vb = [Res(), Res()]
        R_vbc = [[Res() for _ in range(8)] for _ in range(2)]

        def vb_res(par, t0, ntl):
            return [R_vbc[par][c] for c in sorted({t // 4 for t in range(t0, t0 + ntl)})]
        R_acc = Res()
        R_accc = [Res() for _ in range(4)]
        R_pr = [[Res(), Res()] for _ in range(3)]
        R_rsc = [Res(), Res()]
        R_th = [Res(), Res()]
        R_vts = [Res(), Res()]

        def q_rhs(g, quad):
            if g == 0:
                return 64 + 32 * quad, [[8, 4], [256, 16], [1, 8]]
            if g == 1:
                return quad * 256 + 64, [[32, 4], [1024, 4], [1, 32]]
            return 4 * quad * 256 + 64, [[256, 4], [1, 128]]

        def k_chunks(g):
            out = []
            if g == 0:
                for c in range(4):
                    out.append((60 + 32 * c, [[8, 4], [256, 16], [1, 8]], 4, 4 * c))
                out.append((60 + 128, [[256, 16], [1, 8]], 1, 16))
            elif g == 1:
                for r4 in range(4):
                    out.append((r4 * 256 + 48, [[32, 4], [1024, 4], [1, 32]], 4, r4 * 5))
                    out.append((r4 * 256 + 48 + 128, [[1024, 4], [1, 32]], 1, r4 * 5 + 4))
            else:
                for c in range(8):
                    out.append((2 * c * 256, [[256, 2], [1, 256]], 4, 4 * c))
            return out

        NKT = (17, 20, 32)
        pbank = {"i": 0}

        pb_ring = {"r": (0, 1, 4, 5)}

        def next_pbank():
            pbank["i"] += 1
            r = pb_ring["r"]
            return r[pbank["i"] % len(r)]

        vts_ctr = {"i": 0}

        def proj_pieces(g, par, sq, sk, sv):
            pieces = []

            def ones_piece():
                nt = NKT[g]
                vb = vbV[par]
                op("dve", lambda e: e.memset(vbV32[par].ap(32, [[VB_T // 2, nt], [1, 32]]), 1.0019378662109375),
                   writes=R_vbc[par])
                if g == 0:
                    first, last, nch, clen = 0, 16, 1, 17
                elif g == 1:
                    first, last, nch, clen = 0, 4, 4, 5
                else:
                    first, last, nch, clen = 0, 1, 16, 2
                op("dve", lambda e: e.tensor_copy(out=vb.ap(64 + VB_T * first, [[VB_T * clen, nch], [1, 64]]),
                                                   in_=patt.ap(g * 64, [[0, nch], [1, 64]])),
                   reads=[R_patt], writes=R_vbc[par])
                op("dve", lambda e: e.tensor_copy(out=vb.ap(64 + VB_T * last, [[VB_T * clen, nch], [1, 64]]),
                                                   in_=patt.ap((3 + g) * 64, [[0, nch], [1, 64]])),
                   reads=[R_patt], writes=R_vbc[par])
            pieces.append(ones_piece)

            if g == 0:
                for c in range(4):
                    def qpiece0(c=c):
                        bk = next_pbank()
                        proj_fm(bk, sq, 4 * c * 256 + 64, [[256, 4], [1, 128]], 512)
                        copy_op("act", qtV[par].ap(32 * c, [[8, 4], [128, 16], [1, 8]]),
                                bank(bk).ap(0, [[128, 4], [8, 16], [1, 8]]),
                                reads=[R_bank[bk]], writes=[R_qt[par]])
                    pieces.append(qpiece0)
                kgroups = [(r0, 3) for r0 in range(0, 15, 3)] + [(15, 1)]
                kps, vps, tps_ = [], [], []
                for (r0, nr) in kgroups:
                    def vpiece0(r0=r0, nr=nr):
                        bk = next_pbank()
                        proj_fm(bk, sv, r0 * 256 + 60, [[256, nr], [1, 136]], nr * 136)
                        copy_op("act", vt0V.ap(8 * r0, [[8, nr], [128, 17], [1, 8]]),
                                bank(bk).ap(0, [[136, nr], [8, 17], [1, 8]]),
                                reads=[R_bank[bk]], writes=[R_vts[0], R_vts[1]])
                    vps.append(vpiece0)

                    def kpiece0(r0=r0, nr=nr):
                        bk = next_pbank()
                        proj_fm(bk, sk, r0 * 256 + 60, [[256, nr], [1, 136]], nr * 136)
                        copy_op("act", ktV[par].ap(8 * r0, [[8, nr], [128, 17], [1, 8]]),
                                bank(bk).ap(0, [[136, nr], [8, 17], [1, 8]]),
                                reads=[R_bank[bk]], writes=[R_kt[par]])
                    kps.append(kpiece0)
                for t0 in range(0, 17, 4):
                    def tpiece0(t0=t0):
                        ntl = min(4, 17 - t0)
                        vb = vbV[par]
                        bk2 = next_pbank()
                        pb = bank_bf(bk2)
                        fns = []
                        for tt in range(ntl):
                            fns.append(lambda e, tt=tt: e.transpose(out=pb.cols(tt * 128, 128),
                                                                    in_=vt0V.cols((t0 + tt) * 128, 128),
                                                                    identity=ident.base))
                        pe_group(fns, reads=[R_vts[0], R_vts[1], R_const], writes=[R_bank[bk2]])
                        op("dve", lambda e: e.tensor_copy(out=vb.ap(t0 * VB_T, [[VB_T, ntl], [128, 2], [1, 64]]),
                                                          in_=pb.ap(0, [[128, ntl], [64, 2], [1, 64]])),
                           reads=[R_bank[bk2]], writes=vb_res(par, t0, ntl))
                    tps_.append(tpiece0)
                pieces.extend(vps)
                for i_ in range(6):
                    pieces.append(kps[i_])
                    if i_ >= 1:
                        pieces.append(tps_[i_ - 1])
                return pieces

            for quad in range(4):
                def qpiece(quad=quad):
                    st, dims = q_rhs(g, quad)
                    bk = next_pbank()
                    proj_fm(bk, sq, st, dims, 512)
                    copy_op("act", qtV[par].cols(quad * 512, 512), bank(bk).cols(0, 512),
                            reads=[R_bank[bk]], writes=[R_qt[par]])
                pieces.append(qpiece)
            deferred = []
            kpl, vpl = [], []

            def with_deferred(fn):
                def run():
                    due = [d_ for (age, d_) in deferred if age >= 1]
                    rest = [(age + 1, d_) for (age, d_) in deferred if age < 1]
                    del deferred[:]
                    deferred.extend(rest)
                    for d_ in due:
                        d_()
                    fn()
                return run
            for (st, dims, ntl, t0) in k_chunks(g):
                def kpiece(st=st, dims=dims, ntl=ntl, t0=t0):
                    bk = next_pbank()
                    proj_fm(bk, sk, st, dims, ntl * 128)
                    copy_op("act", ktV[par].cols(t0 * 128, ntl * 128), bank(bk).cols(0, ntl * 128),
                            reads=[R_bank[bk]], writes=[R_kt[par]])
                kpl.append(with_deferred(kpiece))

                def vpiece(st=st, dims=dims, ntl=ntl, t0=t0):
                    vb = vbV[par]
                    if g == 2:
                        bk = next_pbank()
                        bv = bank(bk)
                        fns = []
                        for tt in range(ntl):
                            chain, half_ = divmod(t0 + tt, 2)
                            p0 = chain * 256 + half_ * 128
                            for kt in range(8):
                                fns.append(lambda e, tt=tt, kt=kt, p0=p0: e.matmul(
                                    bv.cols(tt * 128, 128), lhsT=hnT.cols(kt * WIN + p0, 128),
                                    rhs=w_lhsT(sv, kt), start=(kt == 0), stop=(kt == 7)))
                        pe_group(fns, reads=[R_wbf[sv], R_hnT], writes=[R_bank[bk]])
                        op("dve", lambda e: e.tensor_copy(out=vb.ap(t0 * VB_T, [[VB_T, ntl], [128, 2], [1, 64]]),
                                                          in_=bv.ap(0, [[128, ntl], [64, 2], [1, 64]])),
                           reads=[R_bank[bk]], writes=vb_res(par, t0, ntl))
                    else:
                        bk = next_pbank()
                        proj_fm(bk, sv, st, dims, ntl * 128)
                        vs = vts_ctr["i"] % 2
                        vts_ctr["i"] += 1
                        copy_op("act", vtsV[vs].cols(0, ntl * 128), bank(bk).cols(0, ntl * 128),
                                reads=[R_bank[bk]], writes=[R_vts[vs]])

                        def vpiece_b(vs=vs, ntl=ntl, t0=t0):
                            bk2 = next_pbank()
                            pb = bank_bf(bk2)
                            fns = []
                            for tt in range(ntl):
                                fns.append(lambda e, tt=tt: e.transpose(out=pb.cols(tt * 128, 128),
                                                                        in_=vtsV[vs].cols(tt * 128, 128),
                                                                        identity=ident.base))
                            pe_group(fns, reads=[R_vts[vs], R_const], writes=[R_bank[bk2]])
                            op("dve", lambda e: e.tensor_copy(out=vb.ap(t0 * VB_T, [[VB_T, ntl], [128, 2], [1, 64]]),
                                                              in_=pb.ap(0, [[128, ntl], [64, 2], [1, 64]])),
                               reads=[R_bank[bk2]], writes=vb_res(par, t0, ntl))
                        deferred.append((0, vpiece_b))
                vpl.append(with_deferred(vpiece))
            pieces.extend(kpl)
            pieces.extend(vpl)

            def flush():
                pend = [d_ for (_, d_) in deferred]
                del deferred[:]
                for d_ in pend:
                    d_()
            pieces.append(flush)
            return pieces

        sbank = {"i": 0}
        obank = {"i": 0}
        R_oh = [[Res(), Res()], [Res(), Res()]]
        prc = {"i": 0}

        s_alt = {"on": False, "i": 0}

        def att_stages(g, par, hp, es_):
            items = []
            for quad in range(4):
                for j in range(2):
                    st = {}

                    def tiles(bi, quad=quad):
                        if g == 0:
                            lo = 4 * quad + bi
                        elif g == 1:
                            lo = quad * 5 + bi
                        else:
                            lo = 2 * (4 * quad + bi)
                        return lo, lo + 1

                    def stage_a(quad=quad, j=j, st=st, tiles=tiles):
                        pslot = prc["i"] % 3
                        prc["i"] += 1
                        st["pslot"] = pslot
                        prv = prV[pslot]
                        sb = 2
                        if s_alt["on"]:
                            sb = 2 if s_alt["i"] % 2 == 0 else 4
                            s_alt["i"] += 1
                        fns = []
                        for b2 in range(2):
                            bi = 2 * j + b2
                            lo, hi = tiles(bi)
                            for t, kt_ in enumerate((lo, hi)):
                                for hh in range(2):
                                    bv = bank(sb + hh)
                                    fns.append(lambda e, b2=b2, t=t, kt_=kt_, bi=bi, bv=bv, hh=hh: e.matmul(
                                        bv.cols(b2 * 256 + t * 128, 128),
                                        lhsT=ktV[par].cols(kt_ * 128, 128, 64 * hh, 64),
                                        rhs=qtV[par].cols((quad * 4 + bi) * 128, 128, 64 * hh, 64),
                                        start=True, stop=True))
                        pe_group(fns, reads=[R_kt[par], R_qt[par]], writes=[R_bank[sb], R_bank[sb + 1]])
                        for hh in range(2):
                            bv = bank(sb + hh)
                            op("act", lambda e, bv=bv, hh=hh: e.activation(
                                out=prv.cols(hh * 512, 512), in_=bv.cols(0, 512), func=AF.Exp, scale=0.125),
                               reads=[R_bank[sb + hh]], writes=[R_pr[pslot][hh]])
                            eoff = es_ * 1536 + (g * 2 + hh) * 256
                            op("dve", lambda e, hh=hh, eoff=eoff: e.tensor_tensor(
                                out=prv.ap(hh * 512, [[256, 2], [1, 256]]),
                                in0=prv.ap(hh * 512, [[256, 2], [1, 256]]),
                                in1=etv.ap(eoff, [[0, 2], [1, 256]]), op=ALU.mult),
                               reads=[R_pr[pslot][hh], R_et[es_]], writes=[R_pr[pslot][hh]])

                    def stage_b(quad=quad, j=j, st=st, tiles=tiles):
                        pslot = st["pslot"]
                        prv = prV[pslot]
                        ob = 6 + (obank["i"] % 2)
                        obank["i"] += 1
                        bv = bank(ob)
                        fns = []
                        for hh in range(2):
                            for b2 in range(2):
                                bi = 2 * j + b2
                                lo, hi = tiles(bi)
                                for t, kt_ in enumerate((lo, hi)):
                                    fns.append(lambda e, b2=b2, t=t, kt_=kt_, hh=hh: e.matmul(
                                        bv.cols(hh * 256 + b2 * 128, 128),
                                        lhsT=vbV[par].cols(kt_ * VB_T + 64 * hh, 128),
                                        rhs=prv.cols(hh * 512 + b2 * 256 + t * 128, 128),
                                        start=(t == 0), stop=(t == 1)))
                        used = set()
                        for b2 in range(2):
                            lo_, hi_ = tiles(2 * j + b2)
                            used.add(lo_ // 4)
                            used.add(hi_ // 4)
                        pe_group(fns, reads=R_pr[pslot] + [R_vbc[par][c] for c in sorted(used)],
                                 writes=[R_bank[ob]])
                        if g == 2:
                            av = accV.ap(quad * 512 + 2 * j * 128, [[NOWN, 2], [1, 256]])
                            ov = bv.ap(0, [[256, 2], [1, 256]])
                            op("dve", lambda e: e.tensor_tensor(out=av, in0=ov, in1=av, op=ALU.add),
                               reads=[R_bank[ob], R_accc[quad]], writes=[R_accc[quad]])
                            return
                        for hh in range(2):
                            hb_ = hh * NOWN
                            if g == 0:
                                av = accV.ap(hb_ + 32 * quad + 16 * j, [[8, 2], [128, 16], [1, 8]])
                                ov = bv.ap(hh * 256, [[128, 2], [8, 16], [1, 8]])
                                op("act", lambda e, av=av, ov=ov: e.activation(out=av, in_=ov, func=AF.Copy),
                                   reads=[R_bank[ob]], writes=R_accc)
                            else:
                                av = accV.ap(hb_ + quad * 128 + 64 * j, [[32, 2], [512, 4], [1, 32]])
                                ov = bv.ap(hh * 256, [[128, 2], [32, 4], [1, 32]])
                                op("dve", lambda e, av=av, ov=ov: e.tensor_tensor(out=av, in0=ov, in1=av, op=ALU.add),
                                   reads=[R_bank[ob]] + R_accc, writes=R_accc)
                    items.append((stage_a, stage_b))
            return items

        fin_ctr = {"i": 0}

        def finalize_parts(hp, sza, cc):
            k = fin_ctr["i"] % 2
            fin_ctr["i"] += 1
            th, rsc = thV[k], rscV[k]
            ca = cc * 512
            cb = NOWN + cc * 512

            def part1():
                op("act", lambda e: e.activation(out=rsc.cols(0, 512, 0, 64), in_=accV.cols(ca, 512, 64, 64),
                                                 func=AF.Copy), reads=[R_accc[cc]], writes=[R_rsc[k]])
                op("act", lambda e: e.activation(out=rsc.cols(0, 512, 64, 64), in_=accV.cols(cb, 512, 0, 64),
                                                 func=AF.Copy), reads=[R_accc[cc]], writes=[R_rsc[k]])
                bk = next_pbank()
                proj_fm(bk, sza, 4 * cc * 256 + 64, [[256, 4], [1, 128]], 512)
                bv = bank(bk)
                op("act", lambda e: e.activation(out=th.base, in_=bv.cols(0, 512), func=AF.Tanh, scale=0.5),
                   reads=[R_bank[bk]], writes=[R_th[k]])
                op("dve", lambda e: e.reciprocal(out=rsc.cols(0, 256), in_=rsc.cols(0, 256)),
                   reads=[R_rsc[k]], writes=[R_rsc[k]])
                op("dve", lambda e: e.scalar_tensor_tensor(out=th.base, in0=th.base, scalar=1.0,
                                                           in1=bv.cols(0, 512), op0=ALU.add, op1=ALU.mult),
                   reads=[R_bank[bk], R_th[k]], writes=[R_th[k]])

            def part2():
                op("dve", lambda e: e.reciprocal(out=rsc.cols(256, 256), in_=rsc.cols(256, 256)),
                   reads=[R_rsc[k]], writes=[R_rsc[k]])
                op("dve", lambda e: e.tensor_tensor(out=rsc.cols(0, 512, 0, 64), in0=accV.cols(ca, 512, 0, 64),
                                                    in1=rsc.cols(0, 512, 0, 64), op=ALU.mult),
                   reads=[R_accc[cc], R_rsc[k]], writes=[R_rsc[k]])
                op("dve", lambda e: e.tensor_tensor(out=rsc.cols(0, 512, 64, 64), in0=accV.cols(cb, 512, 64, 64),
                                                    in1=rsc.cols(0, 512, 64, 64), op=ALU.mult),
                   reads=[R_accc[cc], R_rsc[k]], writes=[R_rsc[k]])
                op("dve", lambda e: e.scalar_tensor_tensor(
                    out=gattn.cols(hp * NOWN + cc * 512, 512), in0=rsc.base, scalar=0.5, in1=th.base,
                    op0=ALU.mult, op1=ALU.mult),
                   reads=[R_rsc[k], R_th[k]], writes=[R_gattn[hp]])
            return [part1, part2]

        def load_etab(hp, es_):
            for g in range(3):
                src = bass.AP(et_d, (g * 8 + 2 * hp) * 128 * 256, [[256, 128], [128 * 256, 2], [1, 256]])
                dst = etv.ap(es_ * 1536 + g * 512, [[256, 2], [1, 256]])
                dma("sp", "d_e%d" % es_, dst, src, writes=[R_et[es_]])

        def phase2(h):
            op("dve", lambda e: e.tensor_scalar(out=patt.cols(0, 192), in0=patc.cols(0, 192),
                                                scalar1=cf.cols(CF_FL + 2 * h, 1), scalar2=1.0,
                                                op0=ALU.mult, op1=ALU.add), reads=[R_const], writes=[R_patt])
            op("dve", lambda e: e.tensor_scalar(out=patt.cols(192, 192), in0=patc.cols(192, 192),
                                                scalar1=cf.cols(CF_FL + 2 * h + 1, 1), scalar2=1.0,
                                                op0=ALU.mult, op1=ALU.add), reads=[R_const], writes=[R_patt])
            units = [(hp, g) for hp in range(4) for g in range(3)]
            slots = {}
            fin_pending = []
            prev_b = [None]

            def take_slots(u):
                hp, g = units[u]
                sq = wnext(C_Q + g * 512 + hp * 128)
                sk = wnext(C_K + g * 512 + hp * 128)
                sv = wnext(C_V + g * 512 + hp * 128)
                slots[u] = (sq, sk, sv)
                if g == 2:
                    slots[("za", hp)] = wnext(C_ZA + hp * 128)

            wahead(3)
            take_slots(0)
            load_etab(0, 0)
            for p in proj_pieces(units[0][1], 0, *slots[0]):
                p()
            for u, (hp, g) in enumerate(units):
                par = u % 2
                es_ = hp % 2
                if g == 0 and hp + 1 < 4:
                    load_etab(hp + 1, (hp + 1) % 2)
                nxt = []
                wahead(4)
                if u + 1 < len(units):
                    take_slots(u + 1)
                    nxt = proj_pieces(units[u + 1][1], (u + 1) % 2, *slots[u + 1])
                last_unit = (u + 1 == len(units))
                s_alt["on"] = last_unit
                if last_unit:
                    pb_ring["r"] = (0, 1)
                items = att_stages(g, par, hp, es_)
                nit = len(items)
                per = (len(nxt) + nit - 1) // nit if nxt else 0
                pi = 0
                def do_b(ib):
                    items[ib][1]()
                    if g == 2 and ib % 2 == 1 and ib < nit - 1:
                        fin_pending.extend(finalize_parts(hp, slots[("za", hp)], ib // 2))

                for i in range(nit):
                    items[i][0]()
                    if fin_pending:
                        fin_pending.pop(0)()
                    if i == 0 and prev_b[0] is not None:
                        prev_b[0]()
                        prev_b[0] = None
                    if g == 0:
                        if i == 2:
                            assert not fin_pending
                            do_b(0)
                            if pi < len(nxt):
                                nxt[pi]()
                                pi += 1
                            do_b(1)
                        elif i > 2:
                            do_b(i - 1)
                    elif i >= 1:
                        do_b(i - 1)
                    for _ in range(per - (1 if (g == 0 and i == 2) else 0)):
                        if pi < len(nxt):
                            nxt[pi]()
                            pi += 1
                while pi < len(nxt):
                    nxt[pi]()
                    pi += 1

                def last_b(items=items, g=g, hp=hp):
                    items[nit - 1][1]()
                    if g == 2:
                        fin_pending.extend(finalize_parts(hp, slots[("za", hp)], 3))
                prev_b[0] = last_b
            prev_b[0]()
            prev_b[0] = None
            s_alt["on"] = False
            return fin_pending

        R_u = Res()
        R_ca = Res()
        R_gz = Res()
        R_hcs = [Res(), Res()]
        R_thb = [Res(), Res()]
        c3a = {"i": 0}

        def phase3a(h, fin_pending):
            def load_wao_wco():
                dma("pool", "d_wao", waoV.ap(0, [[D, 4], [1, D]]), bass.AP(w_ao, 0, [[D, 128], [128 * D, 4], [1, D]]),
                    writes=[R_wao] + R_vbc[1])
                dma("pool", "d_wco", wcoV.ap(0, [[D, 4], [1, D]]), bass.AP(w_co, 0, [[D, 128], [128 * D, 4], [1, D]]),
                    writes=[R_wco] + R_accc + R_vbc[1])
            wloaded = [False]
            al_a = R_qt + R_kt + R_vbc[0]
            for ct in range(4):
                s_hc = wnext(C_HC + ct * 128)
                s_gc = wnext(C_GC + ct * 128)
                s_gb = wnext(C_GB + ct * 128)
                s_zb = wnext(C_ZB + ct * 128)
                wahead(3 if (ct == 0 and fin_pending) else 4)
                cw0, cw1, cw2 = (cf.cols(CF_CW + ct * 3 + k, 1) for k in range(3))
                for cc in range(4):
                    if fin_pending:
                        fin_pending.pop(0)()
                    elif not wloaded[0]:
                        pb_ring["r"] = (0, 1, 4, 5)
                        load_wao_wco()
                        wahead(4)
                        wloaded[0] = True
                    k = c3a["i"] % 2
                    c3a["i"] += 1
                    b0 = 4 * k
                    st_, dims = 4 * cc * 256 + 64, [[256, 4], [1, 128]]
                    proj_fm(b0 + 0, s_hc, st_, dims, 512)
                    proj_fm(b0 + 1, s_gc, st_, dims, 512)
                    proj_fm(b0 + 2, s_gb, st_, dims, 512)
                    proj_fm(b0 + 3, s_zb, st_, dims, 512)
                    hcs, thb = hcsV[k], thbV[k]
                    op("act", lambda e, hcs=hcs, b0=b0: e.activation(out=hcs.base, in_=bank(b0).cols(0, 512), func=AF.Copy),
                       reads=[R_bank[b0]], writes=[R_hcs[k]] + al_a)
                    op("dve", lambda e, hcs=hcs, b0=b0, cc=cc: e.tensor_tensor(
                        out=uV.ap(4 * cc * 130 + 1, [[130, 4], [1, 128]]), in0=bank(b0 + 1).ap(0, [[128, 4], [1, 128]]),
                        in1=hcs.ap(0, [[128, 4], [1, 128]]), op=ALU.mult),
                       reads=[R_bank[b0 + 1], R_hcs[k]], writes=[R_u] + al_a)
                    op("act", lambda e, thb=thb, b0=b0: e.activation(out=thb.base, in_=bank(b0 + 3).cols(0, 512),
                                                                     func=AF.Tanh, scale=0.5),
                       reads=[R_bank[b0 + 3]], writes=[R_thb[k]] + al_a)
                    op("dve", lambda e, thb=thb, b0=b0: e.scalar_tensor_tensor(
                        out=thb.base, in0=thb.base, scalar=1.0, in1=bank(b0 + 3).cols(0, 512), op0=ALU.add, op1=ALU.mult),
                       reads=[R_bank[b0 + 3], R_thb[k]], writes=[R_thb[k]])
                    op("dve", lambda e, thb=thb, b0=b0, cc=cc: e.tensor_tensor(
                        out=gzV.cols(cc * 512, 512), in0=bank(b0 + 2).cols(0, 512), in1=thb.base, op=ALU.mult),
                       reads=[R_bank[b0 + 2], R_thb[k]], writes=[R_gz] + al_a)
                k = c3a["i"] % 2
                c3a["i"] += 1
                b0 = 4 * k
                proj_fm(b0 + 0, s_hc, 192, [[3711, 2]], 2)
                proj_fm(b0 + 1, s_gc, 192, [[3711, 2]], 2)
                hcs = hcsV[k]
                op("act", lambda e, hcs=hcs, b0=b0: e.activation(out=hcs.cols(0, 2), in_=bank(b0).cols(0, 2), func=AF.Copy),
                   reads=[R_bank[b0]], writes=[R_hcs[k]])
                op("dve", lambda e, hcs=hcs, b0=b0: e.tensor_tensor(
                    out=uV.ap(129, [[1821, 2]]), in0=bank(b0 + 1).cols(0, 2), in1=hcs.cols(0, 2), op=ALU.mult),
                   reads=[R_bank[b0 + 1], R_hcs[k]], writes=[R_u])
                U3 = lambda r0, nr, c0: uV.ap(r0 * 130 + c0, [[130, nr], [1, 128]])
                A3 = lambda r0, nr: caV.ap(r0 * 128, [[128, nr], [1, 128]])
                op("dve", lambda e, cw1=cw1: e.tensor_scalar(out=A3(0, 16), in0=U3(0, 16, 1), scalar1=cw1, scalar2=None,
                                                             op0=ALU.mult), reads=[R_u, R_const], writes=[R_ca] + al_a)
                op("dve", lambda e, cw0=cw0: e.scalar_tensor_tensor(out=A3(1, 15), in0=U3(0, 15, 1), scalar=cw0, in1=A3(1, 15),
                                                                    op0=ALU.mult, op1=ALU.add),
                   reads=[R_u, R_ca, R_const], writes=[R_ca])
                op("dve", lambda e, cw0=cw0: e.scalar_tensor_tensor(out=A3(0, 1), in0=U3(15, 1, 0), scalar=cw0, in1=A3(0, 1),
                                                                    op0=ALU.mult, op1=ALU.add),
                   reads=[R_u, R_ca, R_const], writes=[R_ca])
                op("dve", lambda e, cw2=cw2: e.scalar_tensor_tensor(out=A3(0, 15), in0=U3(1, 15, 1), scalar=cw2, in1=A3(0, 15),
                                                                    op0=ALU.mult, op1=ALU.add),
                   reads=[R_u, R_ca, R_const], writes=[R_ca])
                op("dve", lambda e, cw2=cw2: e.scalar_tensor_tensor(out=A3(15, 1), in0=U3(0, 1, 2), scalar=cw2, in1=A3(15, 1),
                                                                    op0=ALU.mult, op1=ALU.add),
                   reads=[R_u, R_ca, R_const], writes=[R_ca])
                op("dve", lambda e, ct=ct: e.scalar_tensor_tensor(out=gconv.cols(ct * NOWN, NOWN), in0=caV.base, scalar=0.5,
                                                                  in1=gzV.base, op0=ALU.mult, op1=ALU.mult),
                   reads=[R_ca, R_gz], writes=[R_gconv[ct]])

        R_tga = [Res(), Res()]
        R_tgc = [Res(), Res()]
        c3b = {"i": 0}

        def phase3b(h):
            al_t = list(R_accc)
            al_m = [R_u, R_ca, R_gz] + R_hcs + R_thb
            for dt_ in range(8):
                s_ga = wnext(C_G + dt_ * 128)
                s_gc = wnext(C_G + 1024 + dt_ * 128)
                wahead(4)
                def chunk_parts(cc, dt_=dt_, s_ga=s_ga, s_gc=s_gc):
                    k = c3b["i"] % 2
                    c3b["i"] += 1
                    b0 = 4 * k
                    st_, dims = 4 * cc * 256 + 64, [[256, 4], [1, 128]]
                    tga, tgc = tgaV[k], tgcV[k]

                    def mm(j):
                        wv, gv_, rw, rg = ((waoV, gattn, R_wao, R_gattn), (wcoV, gconv, R_wco, R_gconv))[j]
                        bv = bank(b0 + 2 + j)
                        fns = []
                        for ct in range(4):
                            fns.append(lambda e, ct=ct: e.matmul(
                                bv.cols(0, 512), lhsT=wv.cols(ct * D + dt_ * 128, 128),
                                rhs=gv_.cols(ct * NOWN + cc * 512, 512), start=(ct == 0), stop=(ct == 3)))
                        pe_group(fns, reads=[rw] + rg, writes=[R_bank[b0 + 2 + j]])

                    def p1():
                        proj_fm(b0 + 0, s_ga, st_, dims, 512)
                        proj_fm(b0 + 1, s_gc, st_, dims, 512)
                        mm(0)

                    def p2():
                        mm(1)

                    def post():
                        op("act", lambda e: e.activation(out=tga.base, in_=bank(b0).cols(0, 512), func=AF.Tanh,
                                                         bias=cf.cols(CF_HB + dt_, 1), scale=0.5),
                           reads=[R_bank[b0], R_const], writes=[R_tga[k]] + al_t)
                        op("act", lambda e: e.activation(out=tgc.base, in_=bank(b0 + 1).cols(0, 512), func=AF.Tanh,
                                                         bias=cf.cols(CF_HB + 8 + dt_, 1), scale=0.5),
                           reads=[R_bank[b0 + 1], R_const], writes=[R_tgc[k]] + al_t)
                        op("dve", lambda e: e.scalar_tensor_tensor(
                            out=tga.base, in0=tga.base, scalar=1.0, in1=bank(b0 + 2).cols(0, 512), op0=ALU.add, op1=ALU.mult),
                           reads=[R_bank[b0 + 2], R_tga[k]], writes=[R_tga[k]])
                        op("dve", lambda e: e.scalar_tensor_tensor(
                            out=tgc.base, in0=tgc.base, scalar=1.0, in1=bank(b0 + 3).cols(0, 512), op0=ALU.add, op1=ALU.mult),
                           reads=[R_bank[b0 + 3], R_tgc[k]], writes=[R_tgc[k]])
                        op("dve", lambda e: e.tensor_tensor(
                            out=mergedV.cols(dt_ * NOWN + cc * 512, 512), in0=tga.base, in1=tgc.base, op=ALU.add),
                           reads=[R_tga[k], R_tgc[k]], writes=[R_merged[dt_]] + al_m)
                    return p1, p2, post

                for c0 in (0, 2):
                    pa = chunk_parts(c0)
                    pb_ = chunk_parts(c0 + 1)
                    pa[0]()
                    pb_[0]()
                    pa[1]()
                    pb_[1]()
                    pa[2]()
                    pb_[2]()
                if dt_ == 5:
                    load_wo(range(0, 4))
                if dt_ == 7:
                    load_wo(range(4, 8))

        nblk_half = len(wstate["blocks"]) // 2
        wlimit["i"] = nblk_half - 1
        wahead(3)
        p1_tiles0 = [p1_stages(0, ti) for ti in range(32)]
        first_loads = []
        for ti in range(2):
            first_loads.append(p1_tiles0[ti][0][1])
            p1_tiles0[ti] = p1_tiles0[ti][1:]
        for f_ in first_loads:
            f_()
        load_consts()
        load_gb()
        run_pipes([(p1_tiles0, 1)])
        for h in range(NHALF):
            wlimit["i"] = (h + 1) * nblk_half - 1
            B.barrier()
            fin_left = phase2(h)
            phase3a(h, fin_left)
            phase3b(h)
            B.barrier()
            load_fg()
            pipes = [([p3_stages(h, r16) for r16 in range(16)], 1)]
            if h + 1 < NHALF:
                load_gb()
                for kt2 in range(0, 8, 2):
                    op("act", lambda e, kt2=kt2: e.activation(
                        out=hnT.ap(kt2 * WIN, [[WIN, 2], [LW, 16], [1, 128]]),
                        in_=hnT.ap(kt2 * WIN + 128, [[WIN, 2], [LW, 16], [1, 128]]), func=AF.Copy),
                       reads=[R_hnT], writes=[R_hnT])
                pipes.append(([p1_stages(h + 1, 2 * r + 1) for r in range(16)], 1))
            run_pipes(pipes)
            if h + 1 < NHALF:
                wlimit["i"] = (h + 2) * nblk_half - 1
                wahead(3)
        assert wctr["i"] == len(wstate["blocks"]), (wctr["i"], len(wstate["blocks"]))

        final_waits = dict(B.dma_counts)
        eng_sems = {B.engs[k].semname for k in ("pe", "act", "dve", "pool")}
        needed = {sn: set() for sn in eng_sems}
        for kind in B.engs:
            for (wl, fn, tok, inc) in B.engs[kind].prog:
                for (sname, v) in wl:
                    if sname in needed:
                        needed[sname].add(v)
        for k2 in ("pe", "act", "dve", "pool"):
            E2 = B.engs[k2]
            if E2.count:
                needed[E2.semname].add(E2.count)
        rank = {sn: {c: i + 1 for i, c in enumerate(sorted(vals))} for sn, vals in needed.items()}

        def covering(sname, c):
            return c

        engmap = {"pe": "tensor", "act": "scalar", "dve": "vector", "pool": "gpsimd", "sp": "sync"}
        with nc.Block() as block:
            def make(kind):
                E = B.engs[kind]

                def body(e):
                    for (wl, fn, tok, inc) in E.prog:
                        wl2 = [(sem[sname], rank[sname][v] if sname in rank else v) for (sname, v) in wl]
                        emb = None
                        if EMBED_WAIT and wl2 and kind in EMBED_WAIT and inc != 16:
                            emb = wl2.pop()
                        for (sh, v) in wl2:
                            e.wait_ge(sh, v)
                        ins = fn(e)
                        if emb is not None:
                            ins._wait_ge(emb[0], emb[1])
                        if tok is not None:
                            if tok[0] in rank:
                                if tok[1] in rank[tok[0]]:
                                    ins.then_inc(sem[tok[0]], 1)
                            else:
                                ins.then_inc(sem[tok[0]], inc)
                    if kind == "sp":
                        for sname, v in final_waits.items():
                            e.wait_ge(sem[sname], v)
                        for k2 in ("pe", "act", "dve", "pool"):
                            E2 = B.engs[k2]
                            if E2.count:
                                e.wait_ge(sem[E2.semname], rank[E2.semname][E2.count])
                return body
            for kind in ("sp", "act", "dve", "pe", "pool"):
                getattr(block, engmap[kind])(make(kind))
        B.stats = {sn: (len(v), B.engs[k].count) for k in ("pe", "act", "dve", "pool") for sn, v in needed.items()
                   if sn == B.engs[k].semname}
    return nc


_CACHE = {}


def _core_layout():
    lay = []
    for c in range(4):
        lay.append((0, 4096 * c, 16384))
    for b in range(2):
        for c in range(2):
            lay.append((1 + b, 4096 * c, 8192))
    return lay


def kernel(x_prompt, x_sample, norm_g, w_in, b_gate, conv_w, w_attn_out, w_conv_out, w_o, final_g):
    x_prompt = np.asarray(x_prompt, np.float32)
    x_sample = np.asarray(x_sample, np.float32)
    seqs = [x_prompt[0], x_sample[0], x_sample[1]]
    lay = _core_layout()
    if "nc" not in _CACHE:
        _CACHE["nc"] = build_program()
        _CACHE["etab"] = _etab().reshape(24 * 128, 256)
        _CACHE["pat"] = _patterns()
        _CACHE["idn"] = np.eye(128, dtype=np.float32).astype(ml_dtypes.bfloat16)
    nc = _CACHE["nc"]
    w_in2 = np.ascontiguousarray(np.asarray(w_in, np.float32)[0])
    w_ao = np.ascontiguousarray(np.asarray(w_attn_out, np.float32)[0])
    w_co = np.ascontiguousarray(np.asarray(w_conv_out, np.float32)[0])
    w_o2 = np.ascontiguousarray(np.asarray(w_o, np.float32)[0])
    gvec = np.ascontiguousarray(np.asarray(norm_g, np.float32)[0])
    fgvec = np.ascontiguousarray(np.asarray(final_g, np.float32))
    bg = np.asarray(b_gate, np.float32)[0].reshape(16, 128).T
    cw = np.asarray(conv_w, np.float32)[0].reshape(3, 4, 128).transpose(2, 1, 0).reshape(128, 12)
    cbf = np.ascontiguousarray(np.concatenate([_CACHE["pat"], _CACHE["idn"]], axis=1))
    in_maps = []
    for (sid, t0, S) in lay:
        xs = seqs[sid]
        xwin = np.zeros((XROWS, D), np.float32)
        lo, hi = t0 - 1024, t0 + 5120
        a, b = max(lo, 0), min(hi, S)
        xwin[a - lo:b - lo] = xs[a:b]
        flags = np.zeros((128, 4), np.float32)
        if t0 == 0:
            flags[:, 0] = -1.0
        if t0 + 4096 == S:
            flags[:, 3] = -1.0
        cfp = np.ascontiguousarray(np.concatenate([cw, flags, bg], axis=1).astype(np.float32))
        in_maps.append({"xw": xwin, "w_in": w_in2, "w_ao": w_ao, "w_co": w_co, "w_o": w_o2, "gvec": gvec,
                        "fgvec": fgvec, "cfp": cfp, "cbf": cbf, "etab": _CACHE["etab"]})
    res = run_bass_kernel_spmd(nc, in_maps, core_ids=list(range(NCORES)))
    ys = [np.asarray(r["y"], np.float32) for r in res.results]
    y_prompt = np.concatenate(ys[0:4], axis=0)[None]
    y_sample = np.stack([np.concatenate(ys[4:6], axis=0), np.concatenate(ys[6:8], axis=0)], axis=0)
    return (y_prompt, y_sample)
```

```python
import numpy as np
import ml_dtypes
import concourse.bass as bass
import concourse.mybir as mybir
from concourse.bass_utils import run_bass_kernel_spmd

F32 = mybir.dt.float32
BF16 = mybir.dt.bfloat16
AF = mybir.ActivationFunctionType
ALU = mybir.AluOpType

NCORES = 8
D = 1024
INW = 9216
TOK_CORE = 4096
NHALF = 2
NOWN = 2048
WIN = 4096
LW = 256
XROWS = 6144
EPS = 1e-6
DIL = (1, 4, 16)

C_Q, C_K, C_V, C_ZA, C_HC, C_GB, C_GC, C_ZB, C_G = 0, 1536, 3072, 4608, 5120, 5632, 6144, 6656, 7168

X_BYTES = 84224
NWS = 8
EMBED_WAIT = ("pe", "act", "dve", "pool")
VB_T = 192


def _slopes():
    n = 24
    s = 2.0 ** (-8.0 * np.arange(1, n + 1) / n)
    return s.astype(np.float32).reshape(3, 8)


def _tile_off(g):
    idx = np.arange(128)
    if g == 0:
        return 16 * (idx % 8) + idx // 8
    if g == 1:
        return 4 * (idx % 32) + idx // 32
    return idx


def _etab():
    sl = _slopes()
    out = np.zeros((24, 128, 256), np.float32)
    for g in range(3):
        off = _tile_off(g).astype(np.int64)
        for t, base in enumerate((-64, 64)):
            delta = (off[:, None] + base) - off[None, :]
            ad = np.abs(delta)
            valid = ad <= 64
            for h in range(8):
                e = np.exp((-sl[g, h] * DIL[g]) * ad.astype(np.float32)).astype(np.float32)
                out[g * 8 + h, :, t * 128:(t + 1) * 128] = np.where(valid, e, 0.0)
    return out.astype(ml_dtypes.bfloat16)


def _patterns():
    idx = np.arange(128)
    p = np.zeros((128, 6, 64), np.float32)
    p[:, 0, :] = ((idx % 8) < 4)[:, None]
    p[:, 1, :] = ((idx % 32) < 16)[:, None]
    p[:, 2, :] = (idx < 64)[:, None]
    p[:, 3, :] = ((idx % 8) >= 4)[:, None]
    p[:, 4, :] = ((idx % 32) >= 16)[:, None]
    p[:, 5, :] = (idx >= 64)[:, None]
    return p.astype(ml_dtypes.bfloat16).reshape(128, 6 * 64)


class Res:
    __slots__ = ("ready", "free", "name")

    def __init__(self, name=""):
        self.ready = {}
        self.free = {}
        self.name = name


def _merge(dst, src):
    for k, v in src.items():
        if dst.get(k, 0) < v:
            dst[k] = v


class Eng:
    def __init__(self, name, semname, is_pe=False):
        self.name = name
        self.semname = semname
        self.count = 0
        self.waited = {}
        self.prog = []
        self.is_pe = is_pe


class View:
    def __init__(self, ap2d):
        self.t = ap2d.tensor
        self.off = ap2d.offset
        self.ps = ap2d.ap[0][0]
        self.w = ap2d.ap[1][1]
        self.base = ap2d

    def ap(self, col, dims, p0=0, npart=128):
        return bass.AP(self.t, self.off + p0 * self.ps + col, [[self.ps, npart]] + [list(d) for d in dims])

    def cols(self, c0, n, p0=0, npart=128):
        return self.ap(c0, [[1, n]], p0, npart)


class Builder:
    def __init__(self):
        self.nc = bass.Bass("TRN2", target_bir_lowering=False)
        self.engs = {
            "pe": Eng("pe", "s_pe", True),
            "act": Eng("act", "s_act"),
            "dve": Eng("dve", "s_dve"),
            "pool": Eng("pool", "s_pool"),
            "sp": Eng("sp", "s_sp"),
        }
        self.dma_counts = {}
        self.epoch = {}
        self.semnames = ["s_pe", "s_act", "s_dve", "s_pool"]

    def _waits(self, E, reads, writes, noepoch):
        waits = {} if (noepoch or E.is_pe) else dict(self.epoch)
        for r in reads:
            _merge(waits, r.ready)
        for w in writes:
            _merge(waits, w.ready)
            _merge(waits, w.free)
        wl = []
        for s, v in waits.items():
            if E.is_pe and s == E.semname:
                continue
            if E.waited.get(s, 0) < v:
                wl.append((s, v))
                E.waited[s] = v
        return wl

    def op(self, eng, fn, reads=(), writes=(), signal=True, noepoch=False):
        E = self.engs[eng]
        wl = self._waits(E, reads, writes, noepoch)
        tok = None
        if signal:
            E.count += 1
            tok = (E.semname, E.count)
        E.prog.append((wl, fn, tok, 1))
        if tok:
            for r in reads:
                if r.free.get(tok[0], 0) < tok[1]:
                    r.free[tok[0]] = tok[1]
            for w in writes:
                if w.ready.get(tok[0], 0) < tok[1]:
                    w.ready[tok[0]] = tok[1]
        return tok

    def pe_group(self, fns, reads=(), writes=()):
        E = self.engs["pe"]
        wl = self._waits(E, reads, writes, False)
        E.count += 1
        tok = (E.semname, E.count)
        n = len(fns)
        for i, fn in enumerate(fns):
            E.prog.append((wl if i == 0 else [], fn, tok if i == n - 1 else None, 1))
        for r in reads:
            if r.free.get(tok[0], 0) < tok[1]:
                r.free[tok[0]] = tok[1]
        for w in writes:
            if w.ready.get(tok[0], 0) < tok[1]:
                w.ready[tok[0]] = tok[1]
        return tok

    def dma(self, queue, semname, out, in_, reads=(), writes=(), noepoch=False):
        E = self.engs[queue]
        wl = self._waits(E, reads, writes, noepoch)
        if semname not in self.dma_counts:
            self.dma_counts[semname] = 0
            self.semnames.append(semname)
        self.dma_counts[semname] += 16
        tok = (semname, self.dma_counts[semname])
        E.prog.append((wl, lambda e: e.dma_start(out=out, in_=in_), tok, 16))
        for r in reads:
            if r.free.get(tok[0], 0) < tok[1]:
                r.free[tok[0]] = tok[1]
        for w in writes:
            if w.ready.get(tok[0], 0) < tok[1]:
                w.ready[tok[0]] = tok[1]
        return tok

    def barrier(self):
        ep = {}
        for k in ("pe", "act", "dve", "pool"):
            E = self.engs[k]
            if E.count:
                ep[E.semname] = E.count
        for s, v in self.dma_counts.items():
            ep[s] = v
        self.epoch = ep


def build_program():
    B = Builder()
    nc = B.nc
    op, pe_group, dma = B.op, B.pe_group, B.dma

    xw = nc.dram_tensor("xw", [XROWS, D], F32, kind="ExternalInput")
    w_in = nc.dram_tensor("w_in", [D, INW], F32, kind="ExternalInput")
    w_ao = nc.dram_tensor("w_ao", [512, D], F32, kind="ExternalInput")
    w_co = nc.dram_tensor("w_co", [512, D], F32, kind="ExternalInput")
    w_o = nc.dram_tensor("w_o", [D, D], F32, kind="ExternalInput")
    gvec = nc.dram_tensor("gvec", [D], F32, kind="ExternalInput")
    fgvec = nc.dram_tensor("fgvec", [D], F32, kind="ExternalInput")
    cfp_d = nc.dram_tensor("cfp", [128, 32], F32, kind="ExternalInput")
    cbf_d = nc.dram_tensor("cbf", [128, 512], BF16, kind="ExternalInput")
    et_d = nc.dram_tensor("etab", [24 * 128, 256], BF16, kind="ExternalInput")
    y_d = nc.dram_tensor("y", [TOK_CORE, D], F32, kind="ExternalOutput")

    import contextlib
    es = contextlib.ExitStack()
    with es:
        def sb(name, shape, dt):
            return es.enter_context(nc.sbuf_tensor(name, shape, dt))

        hnT_t = sb("hnT", [128, 8 * WIN], BF16)
        wbf_t = sb("wbf", [128, NWS * 1024], BF16)
        et_t = sb("et", [128, 2 * 1536], BF16)
        gattn_t = sb("gattn", [128, 4 * NOWN], BF16)
        gconv_t = sb("gconv", [128, 4 * NOWN], BF16)
        X_t = sb("X", [128, X_BYTES // 2], BF16)
        cf_t = sb("cf", [128, 64], F32)
        stat_t = sb("stat", [128, 32], F32)
        cbf_t = sb("cbfs", [128, 512], BF16)
        patt_t = sb("patt", [128, 384], BF16)
        ps_t = es.enter_context(nc.psum_tensor("ps", [128, 4096], F32))
        sem = {}
        for sname in ["s_pe", "s_act", "s_dve", "s_pool"]:
            sem[sname] = es.enter_context(nc.semaphore(sname))
        dma_sem_names = (["d_w%d" % i for i in range(NWS)] + ["d_e0", "d_e1"] + ["d_x1%d" % i for i in range(4)] + ["d_x3%d" % i for i in range(5)]
                         + ["d_y%d" % i for i in range(5)] + ["d_wao", "d_wco", "d_c", "d_gb", "d_fg"]
                         + ["d_xb%d" % i for i in range(8)])
        for sname in dma_sem_names:
            sem[sname] = es.enter_context(nc.semaphore(sname))

        hnT = View(hnT_t[:])
        wbf = View(wbf_t[:])
        etv = View(et_t[:])
        gattn = View(gattn_t[:])
        gconv = View(gconv_t[:])
        cf = View(cf_t[:])
        stat = View(stat_t[:])
        patc = View(cbf_t[:, 0:384])
        patt = View(patt_t[:])
        ident = View(cbf_t[:, 384:512])
        PS = View(ps_t[:])

        def xr(boff, nbytes, dt):
            a = X_t[:, boff // 2:(boff + nbytes) // 2]
            if dt == F32:
                a = a.bitcast(F32)
            return View(a)

        def bank(b, nb=1):
            return View(ps_t[:, 512 * b:512 * (b + nb)])

        def bank_bf(b):
            return View(ps_t[:, 512 * b:512 * (b + 1)].bitcast(BF16))

        R_bank = [Res("bank%d" % i) for i in range(8)]
        R_hnT = Res("hnT")
        R_wbf = [Res() for _ in range(NWS)]
        R_et = [Res(), Res()]
        R_gattn = [Res() for _ in range(4)]
        R_gconv = [Res() for _ in range(4)]
        R_const = Res("const")
        R_stat = [Res() for _ in range(8)]
        R_patt = Res()

        mergedV = xr(0, 32768, BF16)
        fgV = xr(32768, 4096, F32)
        gbV = xr(36864, 4096, F32)
        xt3V = [xr(40960 + 4096 * i, 4096, F32) for i in range(5)]
        xt1V = [xr(61440 + 4096 * i, 4096, F32) for i in range(4)]
        xnV = [xr(77824 + 2048 * i, 2048, BF16) for i in range(2)]
        xt1bV = [xr(4096 * i, 4096, F32) for i in range(8)]
        junkV = xr(81920, 2048, BF16)
        qtV = [xr(0 + 4096 * i, 4096, BF16) for i in range(2)]
        ktV = [xr(8192 + 8192 * i, 8192, BF16) for i in range(2)]
        vbV = [xr(24576 + 12288 * i, 12288, BF16) for i in range(2)]
        vbV32 = [xr(24576 + 12288 * i, 12288, F32) for i in range(2)]
        accV = xr(49152, 16384, F32)
        prV = [xr(65536 + 2048 * i, 2048, BF16) for i in range(3)]
        rscV = [xr(71680 + 2048 * i, 2048, F32) for i in range(2)]
        thV = [xr(75776 + 2048 * i, 2048, F32) for i in range(2)]
        vtsV = [xr(79872 + 1024 * i, 1024, BF16) for i in range(2)]
        vt0V = xr(79872, 4352, BF16)
        uV = xr(0, 8320, F32)
        caV = xr(8320, 8192, F32)
        gzV = xr(16512, 8192, F32)
        hcsV = [xr(24704 + 2048 * i, 2048, F32) for i in range(2)]
        thbV = [xr(28800 + 2048 * i, 2048, F32) for i in range(2)]
        waoV = xr(36864, 8192, BF16)
        wcoV = xr(45056, 8192, BF16)
        tgaV = [xr(53248 + 2048 * i, 2048, F32) for i in range(2)]
        tgcV = [xr(57344 + 2048 * i, 2048, F32) for i in range(2)]

        CF_HB, CF_CW, CF_FL, CF_BG, CF_MH = 0, 16, 28, 32, 48

        def load_consts():
            dma("sp", "d_c", View(cbf_t[:]).base, cbf_d.ap(), writes=[R_const])
            dma("sp", "d_c", cf.cols(CF_CW, 32), cfp_d.ap(), writes=[R_const])
            op("dve", lambda e: e.memset(cf.cols(CF_MH, 1), -0.5), writes=[R_const])
            op("dve", lambda e: e.tensor_scalar(out=cf.cols(CF_HB, 16), in0=cf.cols(CF_BG, 16), scalar1=0.5,
                                                scalar2=None, op0=ALU.mult), reads=[R_const], writes=[R_const])

        wstate = {"issued": 0, "blocks": []}

        def wblocks_for_half():
            bl = []
            for hp in range(4):
                for g in range(3):
                    for c in (C_Q, C_K, C_V):
                        bl.append(c + g * 512 + hp * 128)
                bl.append(C_ZA + hp * 128)
            for ct in range(4):
                for c in (C_HC, C_GC, C_GB, C_ZB):
                    bl.append(c + ct * 128)
            for dt_ in range(8):
                bl.append(C_G + dt_ * 128)
                bl.append(C_G + 1024 + dt_ * 128)
            return bl

        wstate["blocks"] = wblocks_for_half() + wblocks_for_half()

        def wprefetch(upto):
            upto = min(upto, len(wstate["blocks"]) - 1)
            while wstate["issued"] <= upto:
                i = wstate["issued"]
                col = wstate["blocks"][i]
                s = i % NWS
                src = bass.AP(w_in, col, [[INW, 128], [128 * INW, 8], [1, 128]])
                dst = wbf.ap(s * 1024, [[128, 8], [1, 128]])
                dma("pool", "d_w%d" % s, dst, src, writes=[R_wbf[s]], noepoch=True)
                wstate["issued"] += 1

        wctr = {"i": 0}

        def wnext(expect_col):
            i = wctr["i"]
            assert wstate["blocks"][i] == expect_col, (i, wstate["blocks"][i], expect_col)
            assert wstate["issued"] > i, "weight block not prefetched"
            assert wstate["issued"] - i <= NWS
            wctr["i"] += 1
            return i % NWS

        wlimit = {"i": 10 ** 9}

        def wahead(n):
            wprefetch(min(wctr["i"] + n - 1, wlimit["i"]))
            assert wstate["issued"] - wctr["i"] <= NWS

        def w_lhsT(s, kt):
            return wbf.cols(s * 1024 + kt * 128, 128)

        evac_flip = {"i": 0}

        def evac_engine():
            evac_flip["i"] += 1
            return "act" if evac_flip["i"] % 2 else "dve"

        def copy_op(eng, out, in_, reads, writes):
            if eng == "act":
                return op("act", lambda e: e.activation(out=out, in_=in_, func=AF.Copy), reads=reads, writes=writes)
            return op("dve", lambda e: e.tensor_copy(out=out, in_=in_), reads=reads, writes=writes)

        def proj_fm(bk, s, rhs_start, rhs_dims, n, extra_reads=()):
            bv = bank(bk)
            fns = []
            for kt in range(8):
                fns.append(lambda e, kt=kt: e.matmul(bv.cols(0, n), lhsT=w_lhsT(s, kt),
                                                     rhs=hnT.ap(kt * WIN + rhs_start, rhs_dims),
                                                     start=(kt == 0), stop=(kt == 7)))
            return pe_group(fns, reads=[R_wbf[s], R_hnT] + list(extra_reads), writes=[R_bank[bk]])

        stat_ctr = {"i": 0}

        def rms_stat(src_ap, src_res, junk_ap, junk_res):
            k = stat_ctr["i"] % 8
            stat_ctr["i"] += 1
            r = R_stat[k]
            c0 = 3 * k
            op("act", lambda e: e.activation(out=junk_ap, in_=src_ap, func=AF.Square,
                                             accum_out=stat.cols(c0, 1)),
               reads=[src_res], writes=[junk_res, r])
            op("pool", lambda e: e.tensor_scalar(out=stat.cols(c0 + 1, 1), in0=stat.cols(c0, 1), scalar1=1.0 / D,
                                                 scalar2=EPS, op0=ALU.mult, op1=ALU.add), reads=[r], writes=[r])
            op("pool", lambda e: e.tensor_tensor(out=stat.cols(c0 + 2, 1), in0=stat.cols(c0 + 1, 1),
                                                 in1=cf.cols(CF_MH, 1), op=ALU.pow), reads=[r, R_const], writes=[r])
            return stat.cols(c0 + 2, 1), r

        R_junk = Res("junk")
        R_xt1 = [Res() for _ in range(4)]
        R_xt1b = [Res() for _ in range(8)]
        R_xn = [Res(), Res()]
        R_xt3 = [Res() for _ in range(5)]
        R_gb = Res()
        R_fg = Res()
        R_merged = [Res() for _ in range(8)]
        R_wo = Res()
        R_wao = Res()
        R_wco = Res()

        p1ctr = {"i": 0}

        def p1_stages(h, ti):
            r16, lb = ti // 2, ti % 2
            i = p1ctr["i"]
            p1ctr["i"] += 1
            if h == 0:
                s = i % 8
                xv, xres, xsem = xt1bV[s], R_xt1b[s], "d_xb%d" % s
            else:
                s = i % 4
                xv, xres, xsem = xt1V[s], R_xt1[s], "d_x1%d" % s
            sn = i % 2
            bk = (i % 4) if h == 0 else 4 + (i % 2)
            st = {}

            def L():
                row0 = 2048 * h + 2048 * lb + r16
                src = bass.AP(xw, row0 * D, [[16 * D, 128], [1, D]])
                dma("sp", xsem, xv.base, src, writes=[xres])

            def S():
                st["rstd"], st["rs"] = rms_stat(xv.base, xres, junkV.base, R_junk)

            def N():
                rstd, rs = st["rstd"], st["rs"]
                op("dve", lambda e: e.scalar_tensor_tensor(out=xnV[sn].base, in0=xv.base, scalar=rstd,
                                                           in1=gbV.base, op0=ALU.mult, op1=ALU.mult),
                   reads=[xres, rs, R_gb], writes=[R_xn[sn]])
                pb = bank_bf(bk)
                fns = []
                for kt in range(8):
                    fns.append(lambda e, kt=kt: e.transpose(out=pb.cols(kt * 128, 128),
                                                            in_=xnV[sn].cols(kt * 128, 128), identity=ident.base))
                pe_group(fns, reads=[R_xn[sn], R_const], writes=[R_bank[bk]])

            def E():
                pb = bank_bf(bk)
                pos0 = r16 * LW + 128 * lb
                copy_op("dve" if (h == 0 and i % 2) else "act", hnT.ap(pos0, [[WIN, 8], [1, 128]]),
                        pb.ap(0, [[128, 8], [1, 128]]), reads=[R_bank[bk]], writes=[R_hnT])
            return [(0, L), (1, S), (3, N), (6, E)] if h == 0 else [(0, L), (1, S), (3, N), (4.5, E)]

        def run_pipes(pipes):
            ev = []
            for pi_, (tiles, rate) in enumerate(pipes):
                for ti, stages in enumerate(tiles):
                    for (skew, fn) in stages:
                        ev.append((ti / rate + skew, pi_, ti, skew, fn))
            ev.sort(key=lambda x: x[:4])
            for e_ in ev:
                e_[4]()

        def load_gb():
            dma("sp", "d_gb", gbV.base, bass.AP(gvec, 0, [[0, 128], [1, D]]), writes=[R_gb])

        def load_fg():
            dma("sp", "d_fg", fgV.base, bass.AP(fgvec, 0, [[0, 128], [1, D]]), writes=[R_fg])

        p3ctr = {"i": 0}

        def load_wo(rng):
            for dt_ in rng:
                src = bass.AP(w_o, dt_ * 128 * D, [[D, 128], [1, D]])
                dma("pool", "d_w%d" % dt_, wbf.cols(dt_ * 1024, 1024), src, writes=[R_wbf[dt_]])

        def p3_stages(h, r16):
            i = p3ctr["i"]
            p3ctr["i"] += 1
            s = i % 5
            b0 = 2 * (i % 4) if h == NHALF - 1 else 2 * (i % 2)
            st = {}

            def LM():
                row0 = 2048 * h + 1024 + r16
                src = bass.AP(xw, row0 * D, [[16 * D, 128], [1, D]])
                dma("sp", "d_x3%d" % s, xt3V[s].base, src, writes=[R_xt3[s]])
                for nh in range(2):
                    bv = bank(b0 + nh)
                    fns = []
                    for dt_ in range(8):
                        fns.append(lambda e, dt_=dt_, nh=nh, bv=bv: e.matmul(
                            bv.cols(0, 512), lhsT=mergedV.cols(dt_ * NOWN + r16 * 128, 128),
                            rhs=wbf.cols(dt_ * 1024 + nh * 512, 512), start=(dt_ == 0), stop=(dt_ == 7)))
                    pe_group(fns, reads=R_merged + R_wbf, writes=[R_bank[b0 + nh]])

            def A():
                b2 = bank(b0, 2)
                op("dve", lambda e: e.scalar_tensor_tensor(out=xt3V[s].base, in0=b2.base, scalar=0.5,
                                                           in1=xt3V[s].base, op0=ALU.mult, op1=ALU.add),
                   reads=[R_bank[b0], R_bank[b0 + 1], R_xt3[s]], writes=[R_xt3[s]])

            def S3():
                st["rstd"], st["rs"] = rms_stat(xt3V[s].base, R_xt3[s], junkV.base, R_junk)

            def F():
                rstd, rs = st["rstd"], st["rs"]
                op("dve", lambda e: e.scalar_tensor_tensor(out=xt3V[s].base, in0=xt3V[s].base, scalar=rstd,
                                                           in1=fgV.base, op0=ALU.mult, op1=ALU.mult),
                   reads=[R_xt3[s], rs, R_fg], writes=[R_xt3[s]])

            def ST():
                orow0 = 2048 * h + r16
                dst = bass.AP(y_d, orow0 * D, [[16 * D, 128], [1, D]])
                dma("sp", "d_y%d" % s, dst, xt3V[s].base, reads=[R_xt3[s]])
            return [(0, LM), (1, A), (2, S3), (3, F), (4, ST)]

        R_qt = [Res(), Res()]
        R_kt = [Res(), Res()]
        R_vb = [Res(), Res()]
        R_vbc = [[Res() for _ in range(8)] for _ in range(2)]

        def vb_res(par, t0, ntl):
            return [R_vbc[par][c] for c in sorted({t // 4 for t in range(t0, t0 + ntl)})]
        R_acc = Res()
        R_accc = [Res() for _ in range(4)]
        R_pr = [[Res(), Res()] for _ in range(3)]
        R_rsc = [Res(), Res()]
        R_th = [Res(), Res()]
        R_vts = [Res(), Res()]

        def q_rhs(g, quad):
            if g == 0:
                return 64 + 32 * quad, [[8, 4], [256, 16], [1, 8]]
            if g == 1:
                return quad * 256 + 64, [[32, 4], [1024, 4], [1, 32]]
            return 4 * quad * 256 + 64, [[256, 4], [1, 128]]

        def k_chunks(g):
            out = []
            if g == 0:
                for c in range(4):
                    out.append((60 + 32 * c, [[8, 4], [256, 16], [1, 8]], 4, 4 * c))
                out.append((60 + 128, [[256, 16], [1, 8]], 1, 16))
            elif g == 1:
                for r4 in range(4):
                    out.append((r4 * 256 + 48, [[32, 4], [1024, 4], [1, 32]], 4, r4 * 5))
                    out.append((r4 * 256 + 48 + 128, [[1024, 4], [1, 32]], 1, r4 * 5 + 4))
            else:
                for c in range(8):
                    out.append((2 * c * 256, [[256, 2], [1, 256]], 4, 4 * c))
            return out

        NKT = (17, 20, 32)
        pbank = {"i": 0}

        pb_ring = {"r": (0, 1, 4, 5)}

        def next_pbank():
            pbank["i"] += 1
            r = pb_ring["r"]
            return r[pbank["i"] % len(r)]

        vts_ctr = {"i": 0}

        def proj_pieces(g, par, sq, sk, sv):
            pieces = []

            def ones_piece():
                nt = NKT[g]
                vb = vbV[par]
                op("dve", lambda e: e.memset(vbV32[par].ap(32, [[VB_T // 2, nt], [1, 32]]), 1.0019378662109375),
                   writes=R_vbc[par])
                if g == 0:
                    first, last, nch, clen = 0, 16, 1, 17
                elif g == 1:
                    first, last, nch, clen = 0, 4, 4, 5
                else:
                    first, last, nch, clen = 0, 1, 16, 2
                op("dve", lambda e: e.tensor_copy(out=vb.ap(64 + VB_T * first, [[VB_T * clen, nch], [1, 64]]),
                                                   in_=patt.ap(g * 64, [[0, nch], [1, 64]])),
                   reads=[R_patt], writes=R_vbc[par])
                op("dve", lambda e: e.tensor_copy(out=vb.ap(64 + VB_T * last, [[VB_T * clen, nch], [1, 64]]),
                                                   in_=patt.ap((3 + g) * 64, [[0, nch], [1, 64]])),
                   reads=[R_patt], writes=R_vbc[par])
            pieces.append(ones_piece)

            if g == 0:
                for c in range(4):
                    def qpiece0(c=c):
                        bk = next_pbank()
                        proj_fm(bk, sq, 4 * c * 256 + 64, [[256, 4], [1, 128]], 512)
                        copy_op("act", qtV[par].ap(32 * c, [[8, 4], [128, 16], [1, 8]]),
                                bank(bk).ap(0, [[128, 4], [8, 16], [1, 8]]),
                                reads=[R_bank[bk]], writes=[R_qt[par]])
                    pieces.append(qpiece0)
                kgroups = [(r0, 3) for r0 in range(0, 15, 3)] + [(15, 1)]
                kps, vps, tps_ = [], [], []
                for (r0, nr) in kgroups:
                    def vpiece0(r0=r0, nr=nr):
                        bk = next_pbank()
                        proj_fm(bk, sv, r0 * 256 + 60, [[256, nr], [1, 136]], nr * 136)
                        copy_op("act", vt0V.ap(8 * r0, [[8, nr], [128, 17], [1, 8]]),
                                bank(bk).ap(0, [[136, nr], [8, 17], [1, 8]]),
                                reads=[R_bank[bk]], writes=[R_vts[0], R_vts[1]])
                    vps.append(vpiece0)

                    def kpiece0(r0=r0, nr=nr):
                        bk = next_pbank()
                        proj_fm(bk, sk, r0 * 256 + 60, [[256, nr], [1, 136]], nr * 136)
                        copy_op("act", ktV[par].ap(8 * r0, [[8, nr], [128, 17], [1, 8]]),
                                bank(bk).ap(0, [[136, nr], [8, 17], [1, 8]]),
                                reads=[R_bank[bk]], writes=[R_kt[par]])
                    kps.append(kpiece0)
                for t0 in range(0, 17, 4):
                    def tpiece0(t0=t0):
                        ntl = min(4, 17 - t0)
                        vb = vbV[par]
                        bk2 = next_pbank()
                        pb = bank_bf(bk2)
                        fns = []
                        for tt in range(ntl):
                            fns.append(lambda e, tt=tt: e.transpose(out=pb.cols(tt * 128, 128),
                                                                    in_=vt0V.cols((t0 + tt) * 128, 128),
                                                                    identity=ident.base))
                        pe_group(fns, reads=[R_vts[0], R_vts[1], R_const], writes=[R_bank[bk2]])
                        op("dve", lambda e: e.tensor_copy(out=vb.ap(t0 * VB_T, [[VB_T, ntl], [128, 2], [1, 64]]),
                                                          in_=pb.ap(0, [[128, ntl], [64, 2], [1, 64]])),
                           reads=[R_bank[bk2]], writes=vb_res(par, t0, ntl))
                    tps_.append(tpiece0)
                pieces.extend(vps)
                for i_ in range(6):
                    pieces.append(kps[i_])
                    if i_ >= 1:
                        pieces.append(tps_[i_ - 1])
                return pieces

            for quad in range(4):
                def qpiece(quad=quad):
                    st, dims = q_rhs(g, quad)
                    bk = next_pbank()
                    proj_fm(bk, sq, st, dims, 512)
                    copy_op("act", qtV[par].cols(quad * 512, 512), bank(bk).cols(0, 512),
                            reads=[R_bank[bk]], writes=[R_qt[par]])
                pieces.append(qpiece)
            deferred = []
            kpl, vpl = [], []

            def with_deferred(fn):
                def run():
                    due = [d_ for (age, d_) in deferred if age >= 1]
                    rest = [(age + 1, d_) for (age, d_) in deferred if age < 1]
                    del deferred[:]
                    deferred.extend(rest)
                    for d_ in due:
                        d_()
                    fn()
                return run
            for (st, dims, ntl, t0) in k_chunks(g):
                def kpiece(st=st, dims=dims, ntl=ntl, t0=t0):
                    bk = next_pbank()
                    proj_fm(bk, sk, st, dims, ntl * 128)
                    copy_op("act", ktV[par].cols(t0 * 128, ntl * 128), bank(bk).cols(0, ntl * 128),
                            reads=[R_bank[bk]], writes=[R_kt[par]])
                kpl.append(with_deferred(kpiece))

                def vpiece(st=st, dims=dims, ntl=ntl, t0=t0):
                    vb = vbV[par]
                    if g == 2:
                        bk = next_pbank()
                        bv = bank(bk)
                        fns = []
                        for tt in range(ntl):
                            chain, half_ = divmod(t0 + tt, 2)
                            p0 = chain * 256 + half_ * 128
                            for kt in range(8):
                                fns.append(lambda e, tt=tt, kt=kt, p0=p0: e.matmul(
                                    bv.cols(tt * 128, 128), lhsT=hnT.cols(kt * WIN + p0, 128),
                                    rhs=w_lhsT(sv, kt), start=(kt == 0), stop=(kt == 7)))
                        pe_group(fns, reads=[R_wbf[sv], R_hnT], writes=[R_bank[bk]])
                        op("dve", lambda e: e.tensor_copy(out=vb.ap(t0 * VB_T, [[VB_T, ntl], [128, 2], [1, 64]]),
                                                          in_=bv.ap(0, [[128, ntl], [64, 2], [1, 64]])),
                           reads=[R_bank[bk]], writes=vb_res(par, t0, ntl))
                    else:
                        bk = next_pbank()
                        proj_fm(bk, sv, st, dims, ntl * 128)
                        vs = vts_ctr["i"] % 2
                        vts_ctr["i"] += 1
                        copy_op("act", vtsV[vs].cols(0, ntl * 128), bank(bk).cols(0, ntl * 128),
                                reads=[R_bank[bk]], writes=[R_vts[vs]])

                        def vpiece_b(vs=vs, ntl=ntl, t0=t0):
                            bk2 = next_pbank()
                            pb = bank_bf(bk2)
                            fns = []
                            for tt in range(ntl):
                                fns.append(lambda e, tt=tt: e.transpose(out=pb.cols(tt * 128, 128),
                                                                        in_=vtsV[vs].cols(tt * 128, 128),
                                                                        identity=ident.base))
                            pe_group(fns, reads=[R_vts[vs], R_const], writes=[R_bank[bk2]])
                            op("dve", lambda e: e.tensor_copy(out=vb.ap(t0 * VB_T, [[VB_T, ntl], [128, 2], [1, 64]]),
                                                              in_=pb.ap(0, [[128, ntl], [64, 2], [1, 64]])),
                               reads=[R_bank[bk2]], writes=vb_res(par, t0, ntl))
                        deferred.append((0, vpiece_b))
                vpl.append(with_deferred(vpiece))
            pieces.extend(kpl)
            pieces.extend(vpl)

            def flush():
                pend = [d_ for (_, d_) in deferred]
                del deferred[:]
                for d_ in pend:
                    d_()
            pieces.append(flush)
            return pieces

        sbank = {"i": 0}
        obank = {"i": 0}
        R_oh = [[Res(), Res()], [Res(), Res()]]
        prc = {"i": 0}

        s_alt = {"on": False, "i": 0}

        def att_stages(g, par, hp, es_):
            items = []
            for quad in range(4):
                for j in range(2):
                    st = {}

                    def tiles(bi, quad=quad):
                        if g == 0:
                            lo = 4 * quad + bi
                        elif g == 1:
                            lo = quad * 5 + bi
                        else:
                            lo = 2 * (4 * quad + bi)
                        return lo, lo + 1

                    def stage_a(quad=quad, j=j, st=st, tiles=tiles):
                        pslot = prc["i"] % 3
                        prc["i"] += 1
                        st["pslot"] = pslot
                        prv = prV[pslot]
                        sb = 2
                        if s_alt["on"]:
                            sb = 2 if s_alt["i"] % 2 == 0 else 4
                            s_alt["i"] += 1
                        fns = []
                        for b2 in range(2):
                            bi = 2 * j + b2
                            lo, hi = tiles(bi)
                            for t, kt_ in enumerate((lo, hi)):
                                for hh in range(2):
                                    bv = bank(sb + hh)
                                    fns.append(lambda e, b2=b2, t=t, kt_=kt_, bi=bi, bv=bv, hh=hh: e.matmul(
                                        bv.cols(b2 * 256 + t * 128, 128),
                                        lhsT=ktV[par].cols(kt_ * 128, 128, 64 * hh, 64),
                                        rhs=qtV[par].cols((quad * 4 + bi) * 128, 128, 64 * hh, 64),
                                        start=True, stop=True))
                        pe_group(fns, reads=[R_kt[par], R_qt[par]], writes=[R_bank[sb], R_bank[sb + 1]])
                        for hh in range(2):
                            bv = bank(sb + hh)
                            op("act", lambda e, bv=bv, hh=hh: e.activation(
                                out=prv.cols(hh * 512, 512), in_=bv.cols(0, 512), func=AF.Exp, scale=0.125),
                               reads=[R_bank[sb + hh]], writes=[R_pr[pslot][hh]])
                            eoff = es_ * 1536 + (g * 2 + hh) * 256
                            op("dve", lambda e, hh=hh, eoff=eoff: e.tensor_tensor(
                                out=prv.ap(hh * 512, [[256, 2], [1, 256]]),
                                in0=prv.ap(hh * 512, [[256, 2], [1, 256]]),
                                in1=etv.ap(eoff, [[0, 2], [1, 256]]), op=ALU.mult),
                               reads=[R_pr[pslot][hh], R_et[es_]], writes=[R_pr[pslot][hh]])

                    def stage_b(quad=quad, j=j, st=st, tiles=tiles):
                        pslot = st["pslot"]
                        prv = prV[pslot]
                        ob = 6 + (obank["i"] % 2)
                        obank["i"] += 1
                        bv = bank(ob)
                        fns = []
                        for hh in range(2):
                            for b2 in range(2):
                                bi = 2 * j + b2
                                lo, hi = tiles(bi)
                                for t, kt_ in enumerate((lo, hi)):
                                    fns.append(lambda e, b2=b2, t=t, kt_=kt_, hh=hh: e.matmul(
                                        bv.cols(hh * 256 + b2 * 128, 128),
                                        lhsT=vbV[par].cols(kt_ * VB_T + 64 * hh, 128),
                                        rhs=prv.cols(hh * 512 + b2 * 256 + t * 128, 128),
                                        start=(t == 0), stop=(t == 1)))
                        used = set()
                        for b2 in range(2):
                            lo_, hi_ = tiles(2 * j + b2)
                            used.add(lo_ // 4)
                            used.add(hi_ // 4)
                        pe_group(fns, reads=R_pr[pslot] + [R_vbc[par][c] for c in sorted(used)],
                                 writes=[R_bank[ob]])
                        if g == 2:
                            av = accV.ap(quad * 512 + 2 * j * 128, [[NOWN, 2], [1, 256]])
                            ov = bv.ap(0, [[256, 2], [1, 256]])
                            op("dve", lambda e: e.tensor_tensor(out=av, in0=ov, in1=av, op=ALU.add),
                               reads=[R_bank[ob], R_accc[quad]], writes=[R_accc[quad]])
                            return
                        for hh in range(2):
                            hb_ = hh * NOWN
                            if g == 0:
                                av = accV.ap(hb_ + 32 * quad + 16 * j, [[8, 2], [128, 16], [1, 8]])
                                ov = bv.ap(hh * 256, [[128, 2], [8, 16], [1, 8]])
                                op("act", lambda e, av=av, ov=ov: e.activation(out=av, in_=ov, func=AF.Copy),
                                   reads=[R_bank[ob]], writes=R_accc)
                            else:
                                av = accV.ap(hb_ + quad * 128 + 64 * j, [[32, 2], [512, 4], [1, 32]])
                                ov = bv.ap(hh * 256, [[128, 2], [32, 4], [1, 32]])
                                op("dve", lambda e, av=av, ov=ov: e.tensor_tensor(out=av, in0=ov, in1=av, op=ALU.add),
                                   reads=[R_bank[ob]] + R_accc, writes=R_accc)
                    items.append((stage_a, stage_b))
            return items

        fin_ctr = {"i": 0}

        def finalize_parts(hp, sza, cc):
            k = fin_ctr["i"] % 2
            fin_ctr["i"] += 1
            th, rsc = thV[k], rscV[k]
            ca = cc * 512
            cb = NOWN + cc * 512

            def part1():
                op("act", lambda e: e.activation(out=rsc.cols(0, 512, 0, 64), in_=accV.cols(ca, 512, 64, 64),
                                                 func=AF.Copy), reads=[R_accc[cc]], writes=[R_rsc[k]])
                op("act", lambda e: e.activation(out=rsc.cols(0, 512, 64, 64), in_=accV.cols(cb, 512, 0, 64),
                                                 func=AF.Copy), reads=[R_accc[cc]], writes=[R_rsc[k]])
                bk = next_pbank()
                proj_fm(bk, sza, 4 * cc * 256 + 64, [[256, 4], [1, 128]], 512)
                bv = bank(bk)
                op("act", lambda e: e.activation(out=th.base, in_=bv.cols(0, 512), func=AF.Tanh, scale=0.5),
                   reads=[R_bank[bk]], writes=[R_th[k]])
                op("dve", lambda e: e.reciprocal(out=rsc.cols(0, 256), in_=rsc.cols(0, 256)),
                   reads=[R_rsc[k]], writes=[R_rsc[k]])
                op("dve", lambda e: e.scalar_tensor_tensor(out=th.base, in0=th.base, scalar=1.0,
                                                           in1=bv.cols(0, 512), op0=ALU.add, op1=ALU.mult),
                   reads=[R_bank[bk], R_th[k]], writes=[R_th[k]])

            def part2():
                op("dve", lambda e: e.reciprocal(out=rsc.cols(256, 256), in_=rsc.cols(256, 256)),
                   reads=[R_rsc[k]], writes=[R_rsc[k]])
                op("dve", lambda e: e.tensor_tensor(out=rsc.cols(0, 512, 0, 64), in0=accV.cols(ca, 512, 0, 64),
                                                    in1=rsc.cols(0, 512, 0, 64), op=ALU.mult),
                   reads=[R_accc[cc], R_rsc[k]], writes=[R_rsc[k]])
                op("dve", lambda e: e.tensor_tensor(out=rsc.cols(0, 512, 64, 64), in0=accV.cols(cb, 512, 64, 64),
                                                    in1=rsc.cols(0, 512, 64, 64), op=ALU.mult),
                   reads=[R_accc[cc], R_rsc[k]], writes=[R_rsc[k]])
                op("dve", lambda e: e.scalar_tensor_tensor(
                    out=gattn.cols(hp * NOWN + cc * 512, 512), in0=rsc.base, scalar=0.5, in1=th.base,
                    op0=ALU.mult, op1=ALU.mult),
                   reads=[R_rsc[k], R_th[k]], writes=[R_gattn[hp]])
            return [part1, part2]

        def load_etab(hp, es_):
            for g in range(3):
                src = bass.AP(et_d, (g * 8 + 2 * hp) * 128 * 256, [[256, 128], [128 * 256, 2], [1, 256]])
                dst = etv.ap(es_ * 1536 + g * 512, [[256, 2], [1, 256]])
                dma("sp", "d_e%d" % es_, dst, src, writes=[R_et[es_]])

        def phase2(h):
            op("dve", lambda e: e.tensor_scalar(out=patt.cols(0, 192), in0=patc.cols(0, 192),
                                                scalar1=cf.cols(CF_FL + 2 * h, 1), scalar2=1.0,
                                                op0=ALU.mult, op1=ALU.add), reads=[R_const], writes=[R_patt])
            op("dve", lambda e: e.tensor_scalar(out=patt.cols(192, 192), in0=patc.cols(192, 192),
                                                scalar1=cf.cols(CF_FL + 2 * h + 1, 1), scalar2=1.0,
                                                op0=ALU.mult, op1=ALU.add), reads=[R_const], writes=[R_patt])
            units = [(hp, g) for hp in range(4) for g in range(3)]
            slots = {}
            fin_pending = []
            prev_b = [None]

            def take_slots(u):
                hp, g = units[u]
                sq = wnext(C_Q + g * 512 + hp * 128)
                sk = wnext(C_K + g * 512 + hp * 128)
                sv = wnext(C_V + g * 512 + hp * 128)
                slots[u] = (sq, sk, sv)
                if g == 2:
                    slots[("za", hp)] = wnext(C_ZA + hp * 128)

            wahead(3)
            take_slots(0)
            load_etab(0, 0)
            for p in proj_pieces(units[0][1], 0, *slots[0]):
                p()
            for u, (hp, g) in enumerate(units):
                par = u % 2
                es_ = hp % 2
                if g == 0 and hp + 1 < 4:
                    load_etab(hp + 1, (hp + 1) % 2)
                nxt = []
                wahead(4)
                if u + 1 < len(units):
                    take_slots(u + 1)
                    nxt = proj_pieces(units[u + 1][1], (u + 1) % 2, *slots[u + 1])
                last_unit = (u + 1 == len(units))
                s_alt["on"] = last_unit
                if last_unit:
                    pb_ring["r"] = (0, 1)
                items = att_stages(g, par, hp, es_)
                nit = len(items)
                per = (len(nxt) + nit - 1) // nit if nxt else 0
                pi = 0
                def do_b(ib):
                    items[ib][1]()
                    if g == 2 and ib % 2 == 1 and ib < nit - 1:
                        fin_pending.extend(finalize_parts(hp, slots[("za", hp)], ib // 2))

                for i in range(nit):
                    items[i][0]()
                    if fin_pending:
                        fin_pending.pop(0)()
                    if i == 0 and prev_b[0] is not None:
                        prev_b[0]()
                        prev_b[0] = None
                    if g == 0:
                        if i == 2:
                            assert not fin_pending
                            do_b(0)
                            do_b(1)
                        elif i > 2:
                            do_b(i - 1)
                    elif i >= 1:
                        do_b(i - 1)
                    for _ in range(per):
                        if pi < len(nxt):
                            nxt[pi]()
                            pi += 1
                while pi < len(nxt):
                    nxt[pi]()
                    pi += 1

                def last_b(items=items, g=g, hp=hp):
                    items[nit - 1][1]()
                    if g == 2:
                        fin_pending.extend(finalize_parts(hp, slots[("za", hp)], 3))
                prev_b[0] = last_b
            prev_b[0]()
            prev_b[0] = None
            s_alt["on"] = False
            return fin_pending

        R_u = Res()
        R_ca = Res()
        R_gz = Res()
        R_hcs = [Res(), Res()]
        R_thb = [Res(), Res()]
        c3a = {"i": 0}

        def phase3a(h, fin_pending):
            def load_wao_wco():
                dma("pool", "d_wao", waoV.ap(0, [[D, 4], [1, D]]), bass.AP(w_ao, 0, [[D, 128], [128 * D, 4], [1, D]]),
                    writes=[R_wao] + R_vbc[1])
                dma("pool", "d_wco", wcoV.ap(0, [[D, 4], [1, D]]), bass.AP(w_co, 0, [[D, 128], [128 * D, 4], [1, D]]),
                    writes=[R_wco] + R_accc + R_vbc[1])
            wloaded = [False]
            al_a = R_qt + R_kt + R_vbc[0]
            for ct in range(4):
                s_hc = wnext(C_HC + ct * 128)
                s_gc = wnext(C_GC + ct * 128)
                s_gb = wnext(C_GB + ct * 128)
                s_zb = wnext(C_ZB + ct * 128)
                wahead(3 if (ct == 0 and fin_pending) else 4)
                cw0, cw1, cw2 = (cf.cols(CF_CW + ct * 3 + k, 1) for k in range(3))
                for cc in range(4):
                    if fin_pending:
                        fin_pending.pop(0)()
                    elif not wloaded[0]:
                        pb_ring["r"] = (0, 1, 4, 5)
                        load_wao_wco()
                        wahead(4)
                        wloaded[0] = True
                    k = c3a["i"] % 2
                    c3a["i"] += 1
                    b0 = 4 * k
                    st_, dims = 4 * cc * 256 + 64, [[256, 4], [1, 128]]
                    proj_fm(b0 + 0, s_hc, st_, dims, 512)
                    proj_fm(b0 + 1, s_gc, st_, dims, 512)
                    proj_fm(b0 + 2, s_gb, st_, dims, 512)
                    proj_fm(b0 + 3, s_zb, st_, dims, 512)
                    hcs, thb = hcsV[k], thbV[k]
                    op("act", lambda e, hcs=hcs, b0=b0: e.activation(out=hcs.base, in_=bank(b0).cols(0, 512), func=AF.Copy),
                       reads=[R_bank[b0]], writes=[R_hcs[k]] + al_a)
                    op("dve", lambda e, hcs=hcs, b0=b0, cc=cc: e.tensor_tensor(
                        out=uV.ap(4 * cc * 130 + 1, [[130, 4], [1, 128]]), in0=bank(b0 + 1).ap(0, [[128, 4], [1, 128]]),
                        in1=hcs.ap(0, [[128, 4], [1, 128]]), op=ALU.mult),
                       reads=[R_bank[b0 + 1], R_hcs[k]], writes=[R_u] + al_a)
                    op("act", lambda e, thb=thb, b0=b0: e.activation(out=thb.base, in_=bank(b0 + 3).cols(0, 512),
                                                                     func=AF.Tanh, scale=0.5),
                       reads=[R_bank[b0 + 3]], writes=[R_thb[k]] + al_a)
                    op("dve", lambda e, thb=thb, b0=b0: e.scalar_tensor_tensor(
                        out=thb.base, in0=thb.base, scalar=1.0, in1=bank(b0 + 3).cols(0, 512), op0=ALU.add, op1=ALU.mult),
                       reads=[R_bank[b0 + 3], R_thb[k]], writes=[R_thb[k]])
                    op("dve", lambda e, thb=thb, b0=b0, cc=cc: e.tensor_tensor(
                        out=gzV.cols(cc * 512, 512), in0=bank(b0 + 2).cols(0, 512), in1=thb.base, op=ALU.mult),
                       reads=[R_bank[b0 + 2], R_thb[k]], writes=[R_gz] + al_a)
                k = c3a["i"] % 2
                c3a["i"] += 1
                b0 = 4 * k
                proj_fm(b0 + 0, s_hc, 192, [[3711, 2]], 2)
                proj_fm(b0 + 1, s_gc, 192, [[3711, 2]], 2)
                hcs = hcsV[k]
                op("act", lambda e, hcs=hcs, b0=b0: e.activation(out=hcs.cols(0, 2), in_=bank(b0).cols(0, 2), func=AF.Copy),
                   reads=[R_bank[b0]], writes=[R_hcs[k]])
                op("dve", lambda e, hcs=hcs, b0=b0: e.tensor_tensor(
                    out=uV.ap(129, [[1821, 2]]), in0=bank(b0 + 1).cols(0, 2), in1=hcs.cols(0, 2), op=ALU.mult),
                   reads=[R_bank[b0 + 1], R_hcs[k]], writes=[R_u])
                U3 = lambda r0, nr, c0: uV.ap(r0 * 130 + c0, [[130, nr], [1, 128]])
                A3 = lambda r0, nr: caV.ap(r0 * 128, [[128, nr], [1, 128]])
                op("dve", lambda e, cw1=cw1: e.tensor_scalar(out=A3(0, 16), in0=U3(0, 16, 1), scalar1=cw1, scalar2=None,
                                                             op0=ALU.mult), reads=[R_u, R_const], writes=[R_ca] + al_a)
                op("dve", lambda e, cw0=cw0: e.scalar_tensor_tensor(out=A3(1, 15), in0=U3(0, 15, 1), scalar=cw0, in1=A3(1, 15),
                                                                    op0=ALU.mult, op1=ALU.add),
                   reads=[R_u, R_ca, R_const], writes=[R_ca])
                op("dve", lambda e, cw0=cw0: e.scalar_tensor_tensor(out=A3(0, 1), in0=U3(15, 1, 0), scalar=cw0, in1=A3(0, 1),
                                                                    op0=ALU.mult, op1=ALU.add),
                   reads=[R_u, R_ca, R_const], writes=[R_ca])
                op("dve", lambda e, cw2=cw2: e.scalar_tensor_tensor(out=A3(0, 15), in0=U3(1, 15, 1), scalar=cw2, in1=A3(0, 15),
                                                                    op0=ALU.mult, op1=ALU.add),
                   reads=[R_u, R_ca, R_const], writes=[R_ca])
                op("dve", lambda e, cw2=cw2: e.scalar_tensor_tensor(out=A3(15, 1), in0=U3(0, 1, 2), scalar=cw2, in1=A3(15, 1),
                                                                    op0=ALU.mult, op1=ALU.add),
                   reads=[R_u, R_ca, R_const], writes=[R_ca])
                op("dve", lambda e, ct=ct: e.scalar_tensor_tensor(out=gconv.cols(ct * NOWN, NOWN), in0=caV.base, scalar=0.5,
                                                                  in1=gzV.base, op0=ALU.mult, op1=ALU.mult),
                   reads=[R_ca, R_gz], writes=[R_gconv[ct]])

        R_tga = [Res(), Res()]
        R_tgc = [Res(), Res()]
        c3b = {"i": 0}

        def phase3b(h):
            al_t = list(R_accc)
            al_m = [R_u, R_ca, R_gz] + R_hcs + R_thb
            for dt_ in range(8):
                s_ga = wnext(C_G + dt_ * 128)
                s_gc = wnext(C_G + 1024 + dt_ * 128)
                wahead(4)
                def chunk_parts(cc, dt_=dt_, s_ga=s_ga, s_gc=s_gc):
                    k = c3b["i"] % 2
                    c3b["i"] += 1
                    b0 = 4 * k
                    st_, dims = 4 * cc * 256 + 64, [[256, 4], [1, 128]]
                    tga, tgc = tgaV[k], tgcV[k]

                    def mm(j):
                        wv, gv_, rw, rg = ((waoV, gattn, R_wao, R_gattn), (wcoV, gconv, R_wco, R_gconv))[j]
                        bv = bank(b0 + 2 + j)
                        fns = []
                        for ct in range(4):
                            fns.append(lambda e, ct=ct: e.matmul(
                                bv.cols(0, 512), lhsT=wv.cols(ct * D + dt_ * 128, 128),
                                rhs=gv_.cols(ct * NOWN + cc * 512, 512), start=(ct == 0), stop=(ct == 3)))
                        pe_group(fns, reads=[rw] + rg, writes=[R_bank[b0 + 2 + j]])

                    def p1():
                        proj_fm(b0 + 0, s_ga, st_, dims, 512)
                        proj_fm(b0 + 1, s_gc, st_, dims, 512)
                        mm(0)

                    def p2():
                        mm(1)

                    def post():
                        op("act", lambda e: e.activation(out=tga.base, in_=bank(b0).cols(0, 512), func=AF.Tanh,
                                                         bias=cf.cols(CF_HB + dt_, 1), scale=0.5),
                           reads=[R_bank[b0], R_const], writes=[R_tga[k]] + al_t)
                        op("act", lambda e: e.activation(out=tgc.base, in_=bank(b0 + 1).cols(0, 512), func=AF.Tanh,
                                                         bias=cf.cols(CF_HB + 8 + dt_, 1), scale=0.5),
                           reads=[R_bank[b0 + 1], R_const], writes=[R_tgc[k]] + al_t)
                        op("dve", lambda e: e.scalar_tensor_tensor(
                            out=tga.base, in0=tga.base, scalar=1.0, in1=bank(b0 + 2).cols(0, 512), op0=ALU.add, op1=ALU.mult),
                           reads=[R_bank[b0 + 2], R_tga[k]], writes=[R_tga[k]])
                        op("dve", lambda e: e.scalar_tensor_tensor(
                            out=tgc.base, in0=tgc.base, scalar=1.0, in1=bank(b0 + 3).cols(0, 512), op0=ALU.add, op1=ALU.mult),
                           reads=[R_bank[b0 + 3], R_tgc[k]], writes=[R_tgc[k]])
                        op("dve", lambda e: e.tensor_tensor(
                            out=mergedV.cols(dt_ * NOWN + cc * 512, 512), in0=tga.base, in1=tgc.base, op=ALU.add),
                           reads=[R_tga[k], R_tgc[k]], writes=[R_merged[dt_]] + al_m)
                    return p1, p2, post

                for c0 in (0, 2):
                    pa = chunk_parts(c0)
                    pb_ = chunk_parts(c0 + 1)
                    pa[0]()
                    pb_[0]()
                    pa[1]()
                    pb_[1]()
                    pa[2]()
                    pb_[2]()
                if dt_ == 5:
                    load_wo(range(0, 4))
                if dt_ == 7:
                    load_wo(range(4, 8))

        nblk_half = len(wstate["blocks"]) // 2
        wlimit["i"] = nblk_half - 1
        wahead(3)
        p1_tiles0 = [p1_stages(0, ti) for ti in range(32)]
        first_loads = []
        for ti in range(2):
            first_loads.append(p1_tiles0[ti][0][1])
            p1_tiles0[ti] = p1_tiles0[ti][1:]
        for f_ in first_loads:
            f_()
        load_consts()
        load_gb()
        run_pipes([(p1_tiles0, 1)])
        for h in range(NHALF):
            wlimit["i"] = (h + 1) * nblk_half - 1
            B.barrier()
            fin_left = phase2(h)
            phase3a(h, fin_left)
            phase3b(h)
            B.barrier()
            load_fg()
            pipes = [([p3_stages(h, r16) for r16 in range(16)], 1)]
            if h + 1 < NHALF:
                load_gb()
                for kt2 in range(0, 8, 2):
                    op("dve", lambda e, kt2=kt2: e.tensor_copy(
                        out=hnT.ap(kt2 * WIN, [[WIN, 2], [LW, 16], [1, 128]]),
                        in_=hnT.ap(kt2 * WIN + 128, [[WIN, 2], [LW, 16], [1, 128]])),
                       reads=[R_hnT], writes=[R_hnT])
                pipes.append(([p1_stages(h + 1, 2 * r + 1) for r in range(16)], 1))
            run_pipes(pipes)
            if h + 1 < NHALF:
                wlimit["i"] = (h + 2) * nblk_half - 1
                wahead(3)
        assert wctr["i"] == len(wstate["blocks"]), (wctr["i"], len(wstate["blocks"]))

        final_waits = dict(B.dma_counts)
        eng_sems = {B.engs[k].semname for k in ("pe", "act", "dve", "pool")}
        needed = {sn: set() for sn in eng_sems}
        for kind in B.engs:
            for (wl, fn, tok, inc) in B.engs[kind].prog:
                for (sname, v) in wl:
                    if sname in needed:
                        needed[sname].add(v)
        for k2 in ("pe", "act", "dve", "pool"):
            E2 = B.engs[k2]
            if E2.count:
                needed[E2.semname].add(E2.count)
        rank = {sn: {c: i + 1 for i, c in enumerate(sorted(vals))} for sn, vals in needed.items()}

        def covering(sname, c):
            return c

        engmap = {"pe": "tensor", "act": "scalar", "dve": "vector", "pool": "gpsimd", "sp": "sync"}
        with nc.Block() as block:
            def make(kind):
                E = B.engs[kind]

                def body(e):
                    for (wl, fn, tok, inc) in E.prog:
                        wl2 = [(sem[sname], rank[sname][v] if sname in rank else v) for (sname, v) in wl]
                        emb = None
                        if EMBED_WAIT and wl2 and kind in EMBED_WAIT and inc != 16:
                            emb = wl2.pop()
                        for (sh, v) in wl2:
                            e.wait_ge(sh, v)
                        ins = fn(e)
                        if emb is not None:
                            ins._wait_ge(emb[0], emb[1])
                        if tok is not None:
                            if tok[0] in rank:
                                if tok[1] in rank[tok[0]]:
                                    ins.then_inc(sem[tok[0]], 1)
                            else:
                                ins.then_inc(sem[tok[0]], inc)
                    if kind == "sp":
                        for sname, v in final_waits.items():
                            e.wait_ge(sem[sname], v)
                        for k2 in ("pe", "act", "dve", "pool"):
                            E2 = B.engs[k2]
                            if E2.count:
                                e.wait_ge(sem[E2.semname], rank[E2.semname][E2.count])
                return body
            for kind in ("sp", "act", "dve", "pe", "pool"):
                getattr(block, engmap[kind])(make(kind))
        B.stats = {sn: (len(v), B.engs[k].count) for k in ("pe", "act", "dve", "pool") for sn, v in needed.items()
                   if sn == B.engs[k].semname}
    return nc


_CACHE = {}


def _core_layout():
    lay = []
    for c in range(4):
        lay.append((0, 4096 * c, 16384))
    for b in range(2):
        for c in range(2):
            lay.append((1 + b, 4096 * c, 8192))
    return lay


def kernel(x_prompt, x_sample, norm_g, w_in, b_gate, conv_w, w_attn_out, w_conv_out, w_o, final_g):
    x_prompt = np.asarray(x_prompt, np.float32)
    x_sample = np.asarray(x_sample, np.float32)
    seqs = [x_prompt[0], x_sample[0], x_sample[1]]
    lay = _core_layout()
    if "nc" not in _CACHE:
        _CACHE["nc"] = build_program()
        _CACHE["etab"] = _etab().reshape(24 * 128, 256)
        _CACHE["pat"] = _patterns()
        _CACHE["idn"] = np.eye(128, dtype=np.float32).astype(ml_dtypes.bfloat16)
    nc = _CACHE["nc"]
    w_in2 = np.ascontiguousarray(np.asarray(w_in, np.float32)[0])
    w_ao = np.ascontiguousarray(np.asarray(w_attn_out, np.float32)[0])
    w_co = np.ascontiguousarray(np.asarray(w_conv_out, np.float32)[0])
    w_o2 = np.ascontiguousarray(np.asarray(w_o, np.float32)[0])
    gvec = np.ascontiguousarray(np.asarray(norm_g, np.float32)[0])
    fgvec = np.ascontiguousarray(np.asarray(final_g, np.float32))
    bg = np.asarray(b_gate, np.float32)[0].reshape(16, 128).T
    cw = np.asarray(conv_w, np.float32)[0].reshape(3, 4, 128).transpose(2, 1, 0).reshape(128, 12)
    cbf = np.ascontiguousarray(np.concatenate([_CACHE["pat"], _CACHE["idn"]], axis=1))
    in_maps = []
    for (sid, t0, S) in lay:
        xs = seqs[sid]
        xwin = np.zeros((XROWS, D), np.float32)
        lo, hi = t0 - 1024, t0 + 5120
        a, b = max(lo, 0), min(hi, S)
        xwin[a - lo:b - lo] = xs[a:b]
        flags = np.zeros((128, 4), np.float32)
        if t0 == 0:
            flags[:, 0] = -1.0
        if t0 + 4096 == S:
            flags[:, 3] = -1.0
        cfp = np.ascontiguousarray(np.concatenate([cw, flags, bg], axis=1).astype(np.float32))
        in_maps.append({"xw": xwin, "w_in": w_in2, "w_ao": w_ao, "w_co": w_co, "w_o": w_o2, "gvec": gvec,
                        "fgvec": fgvec, "cfp": cfp, "cbf": cbf, "etab": _CACHE["etab"]})
    res = run_bass_kernel_spmd(nc, in_maps, core_ids=list(range(NCORES)))
    ys = [np.asarray(r["y"], np.float32) for r in res.results]
    y_prompt = np.concatenate(ys[0:4], axis=0)[None]
    y_sample = np.stack([np.concatenate(ys[4:6], axis=0), np.concatenate(ys[6:8], axis=0)], axis=0)
    return (y_prompt, y_sample)
```
